# Optimizing a Trainium2 kernel written in Bass

```python
import jax, jax.numpy as jnp
from jax import lax
import numpy as np

D_MODEL = 1024
BATCH = 8
SEQ = 4096
DEPTH = 4

N_META = 16
CHUNK = 64
NORM_EPS = 1e-6
GLA_HEADS = 4
GLA_DK = 128
GLA_DV = 256
GLA_KEY = GLA_HEADS * GLA_DK
GLA_VAL = GLA_HEADS * GLA_DV
GLA_GATE_RANK = 16
GLA_GATE_NORM = 16.0
GLA_LOG_DECAY_MIN = -1.0
DN_HEADS = 8
DN_DK = 128
DN_DV = 128
DN_KEY = DN_HEADS * DN_DK
DN_VAL = DN_HEADS * DN_DV
DN_CONV = 4
D_FF = 4 * D_MODEL
IN_SIZES = (GLA_KEY, GLA_KEY, GLA_VAL, GLA_GATE_RANK, GLA_VAL,
            2 * DN_KEY + DN_VAL, DN_VAL, DN_HEADS, DN_HEADS,
            D_MODEL, D_MODEL)
IN_COLS = sum(IN_SIZES)

kernel_name = "hybrid_gla_gdn_meta_block"


def rms_norm(x, w):
    xf = x.astype(jnp.float32)
    y = xf * lax.rsqrt(jnp.mean(xf * xf, axis=-1, keepdims=True) + NORM_EPS)
    return (y * w.astype(jnp.float32)).astype(x.dtype)


def l2_normalize(x):
    xf = x.astype(jnp.float32)
    return xf * lax.rsqrt(jnp.sum(xf * xf, axis=-1, keepdims=True) + NORM_EPS)


def causal_depthwise_conv(x, w):
    k_width, channels = w.shape
    return lax.conv_general_dilated(
        x, w[:, None, :].astype(x.dtype), window_strides=(1,), padding=[(k_width - 1, 0)],
        dimension_numbers=('NWC', 'WIO', 'NWC'), feature_group_count=channels)


def to_chunks(x, pad):
    x = jnp.pad(x, ((0, 0), (pad, 0), (0, 0), (0, 0)))
    b_, lp, h, d = x.shape
    return x.reshape(b_, lp // CHUNK, CHUNK, h, d).transpose(0, 3, 1, 2, 4)


def from_chunks(o, pad):
    b_, h, n, c, d = o.shape
    return o.transpose(0, 2, 3, 1, 4).reshape(b_, n * c, h, d)[:, pad:]


def gla_chunked(q, k, v, log_a):
    b_, length, h, dk = q.shape
    dv = v.shape[-1]
    pad = (-length) % CHUNK
    qc, kc, vc, gc = (to_chunks(t.astype(jnp.float32), pad) for t in (q, k, v, log_a))
    cum = jnp.cumsum(gc, axis=3)
    cum_last = cum[:, :, :, -1:, :]
    qe = qc * jnp.exp(cum) * (dk ** -0.5)
    ke = kc * jnp.exp(-cum)
    kd = kc * jnp.exp(cum_last - cum)
    idx = jnp.arange(CHUNK)
    incl = idx[:, None] >= idx[None, :]
    scores = jnp.where(incl, jnp.einsum('bhncd,bhnsd->bhncs', qe, ke), 0.0)
    o_intra = jnp.einsum('bhncs,bhnsv->bhncv', scores, vc)

    def step(state, xs):
        qe_n, kd_n, v_n, dec_n = xs
        o_n = jnp.einsum('bhcd,bhdv->bhcv', qe_n, state)
        state = state * dec_n[..., None] + jnp.einsum('bhcd,bhcv->bhdv', kd_n, v_n)
        return state, o_n

    s0 = jnp.zeros((b_, h, dk, dv), jnp.float32)
    xs = tuple(jnp.moveaxis(t, 2, 0) for t in (qe, kd, vc, jnp.exp(cum_last[:, :, :, 0, :])))
    _, o_inter = lax.scan(step, s0, xs)
    return from_chunks(o_intra + jnp.moveaxis(o_inter, 0, 2), pad)


def gated_delta_chunked(q, k, v, beta, log_a):
    b_, length, h, dk = q.shape
    dv = v.shape[-1]
    pad = (-length) % CHUNK
    qc, kc, vc = (to_chunks(t.astype(jnp.float32), pad) for t in (q, k, v))
    bc = to_chunks(beta.astype(jnp.float32)[..., None], pad)[..., 0]
    gc = to_chunks(log_a.astype(jnp.float32)[..., None], pad)[..., 0]
    gam = jnp.cumsum(gc, axis=-1)
    idx = jnp.arange(CHUNK)
    incl = idx[:, None] >= idx[None, :]
    strict = idx[:, None] > idx[None, :]
    decay = jnp.exp(jnp.where(incl, gam[..., :, None] - gam[..., None, :], -jnp.inf))
    kb = kc * bc[..., None]
    a_kk = jnp.where(strict, jnp.einsum('bhncd,bhnsd->bhncs', kb, kc) * decay, 0.0)
    tri = a_kk + jnp.eye(CHUNK, dtype=jnp.float32)
    u = lax.linalg.triangular_solve(tri, vc * bc[..., None], left_side=True, lower=True, unit_diagonal=True)
    w = lax.linalg.triangular_solve(tri, kb * jnp.exp(gam)[..., None], left_side=True, lower=True, unit_diagonal=True)
    qs = qc * (dk ** -0.5)
    a_qk = jnp.einsum('bhncd,bhnsd->bhncs', qs, kc) * decay
    qd = qs * jnp.exp(gam)[..., None]
    kd = kc * jnp.exp(gam[..., -1:] - gam)[..., None]

    def step(state, xs):
        qd_n, kd_n, w_n, u_n, aqk_n, dec_n = xs
        v_new = u_n - jnp.einsum('bhcd,bhdv->bhcv', w_n, state)
        o_n = jnp.einsum('bhcd,bhdv->bhcv', qd_n, state) + jnp.einsum('bhcs,bhsv->bhcv', aqk_n, v_new)
        state = state * dec_n[..., None, None] + jnp.einsum('bhcd,bhcv->bhdv', kd_n, v_new)
        return state, o_n

    s0 = jnp.zeros((b_, h, dk, dv), jnp.float32)
    xs = tuple(jnp.moveaxis(t, 2, 0) for t in (qd, kd, w, u, a_qk, jnp.exp(gam[..., -1])))
    _, o = lax.scan(step, s0, xs)
    return from_chunks(jnp.moveaxis(o, 0, 2), pad)


def hybrid_mixer(u, w_in, gla_w_gate_up, gla_b_gate, gla_norm_w, dn_conv_w, dn_a_log, dn_dt_bias,
                 dn_norm_w, w_branch_gla, w_branch_dn, w_out):
    b_, length, _ = u.shape
    proj = u @ w_in
    split_points = [int(s) for s in np.cumsum(IN_SIZES)[:-1]]
    (g_q, g_k, g_v, g_lr, g_z, d_qkv, d_z, d_b, d_a, gate_gla, gate_dn) = jnp.split(proj, split_points, axis=-1)

    q = g_q.reshape(b_, length, GLA_HEADS, GLA_DK)
    k = g_k.reshape(b_, length, GLA_HEADS, GLA_DK)
    v = g_v.reshape(b_, length, GLA_HEADS, GLA_DV)
    gate_logit = (g_lr @ gla_w_gate_up + gla_b_gate).astype(jnp.float32)
    log_a = jnp.maximum(jax.nn.log_sigmoid(gate_logit) / GLA_GATE_NORM, GLA_LOG_DECAY_MIN)
    o = gla_chunked(q, k, v, log_a.reshape(b_, length, GLA_HEADS, GLA_DK)).astype(u.dtype)
    o = rms_norm(o, gla_norm_w) * jax.nn.silu(g_z.reshape(b_, length, GLA_HEADS, GLA_DV))
    o_gla = o.reshape(b_, length, GLA_VAL)

    qkv = jax.nn.silu(causal_depthwise_conv(d_qkv, dn_conv_w))
    dq, dk_, dv_ = jnp.split(qkv, [DN_KEY, 2 * DN_KEY], axis=-1)
    dq = l2_normalize(dq.reshape(b_, length, DN_HEADS, DN_DK))
    dk_ = l2_normalize(dk_.reshape(b_, length, DN_HEADS, DN_DK))
    dv_ = dv_.reshape(b_, length, DN_HEADS, DN_DV)
    beta = jax.nn.sigmoid(d_b.astype(jnp.float32))
    log_alpha = -jnp.exp(dn_a_log.astype(jnp.float32)) * jax.nn.softplus(
        d_a.astype(jnp.float32) + dn_dt_bias.astype(jnp.float32))
    o = gated_delta_chunked(dq, dk_, dv_, beta, log_alpha).astype(u.dtype)
    o = rms_norm(o, dn_norm_w) * jax.nn.silu(d_z.reshape(b_, length, DN_HEADS, DN_DV))
    o_dn = o.reshape(b_, length, DN_VAL)

    merged = jax.nn.sigmoid(gate_gla) * (o_gla @ w_branch_gla) + jax.nn.sigmoid(gate_dn) * (o_dn @ w_branch_dn)
    return merged @ w_out


def squared_relu_mlp(u, w_up, w_down):
    return jnp.square(jax.nn.relu(u @ w_up)) @ w_down


def setup_inputs(seed: int = 0) -> dict:
    key = jax.random.key(seed)
    ks = jax.random.split(key, 20)
    f32 = jnp.float32

    def nrm(k, shape, scale):
        return jax.random.normal(k, shape, f32) * scale

    def gain(k, shape):
        return 1.0 + 0.02 * jax.random.normal(k, shape, f32)

    dt = jnp.exp(jax.random.uniform(ks[9], (DEPTH, DN_HEADS), f32) * (np.log(0.1) - np.log(1e-3)) + np.log(1e-3))
    return {
        'x': nrm(ks[0], (BATCH, SEQ, D_MODEL), 1.0),
        'meta_tokens': nrm(ks[1], (N_META, D_MODEL), 1.0),
        'mixer_norm_w': gain(ks[2], (DEPTH, D_MODEL)),
        'w_in': nrm(ks[3], (DEPTH, D_MODEL, IN_COLS), D_MODEL ** -0.5),
        'gla_w_gate_up': nrm(ks[4], (DEPTH, GLA_GATE_RANK, GLA_KEY), GLA_GATE_RANK ** -0.5),
        'gla_b_gate': nrm(ks[5], (DEPTH, GLA_KEY), 0.1),
        'gla_norm_w': gain(ks[6], (DEPTH, GLA_DV)),
        'dn_conv_w': nrm(ks[7], (DEPTH, DN_CONV, 2 * DN_KEY + DN_VAL), DN_CONV ** -0.5),
        'dn_a_log': jnp.log(jax.random.uniform(ks[8], (DEPTH, DN_HEADS), f32, 1.0, 16.0)),
        'dn_dt_bias': dt + jnp.log(-jnp.expm1(-dt)),
        'dn_norm_w': gain(ks[10], (DEPTH, DN_DV)),
        'w_branch_gla': nrm(ks[11], (DEPTH, GLA_VAL, D_MODEL), GLA_VAL ** -0.5),
        'w_branch_dn': nrm(ks[12], (DEPTH, DN_VAL, D_MODEL), DN_VAL ** -0.5),
        'w_out': nrm(ks[13], (DEPTH, D_MODEL, D_MODEL), D_MODEL ** -0.5),
        'mlp_norm_w': gain(ks[14], (DEPTH, D_MODEL)),
        'w_mlp_up': nrm(ks[15], (DEPTH, D_MODEL, D_FF), D_MODEL ** -0.5),
        'w_mlp_down': nrm(ks[16], (DEPTH, D_FF, D_MODEL), D_FF ** -0.5),
        'final_norm_w': gain(ks[17], (D_MODEL,)),
    }


def reference(x, meta_tokens, mixer_norm_w, w_in, gla_w_gate_up, gla_b_gate, gla_norm_w, dn_conv_w,
              dn_a_log, dn_dt_bias, dn_norm_w, w_branch_gla, w_branch_dn, w_out, mlp_norm_w,
              w_mlp_up, w_mlp_down, final_norm_w):
    b_ = x.shape[0]
    meta = jnp.broadcast_to(meta_tokens[None].astype(x.dtype), (b_, N_META, x.shape[-1]))
    h = jnp.concatenate([meta, x], axis=1)
    for layer in range(DEPTH):
        h = h + hybrid_mixer(rms_norm(h, mixer_norm_w[layer]), w_in[layer], gla_w_gate_up[layer],
                             gla_b_gate[layer], gla_norm_w[layer], dn_conv_w[layer], dn_a_log[layer],
                             dn_dt_bias[layer], dn_norm_w[layer], w_branch_gla[layer], w_branch_dn[layer],
                             w_out[layer])
        h = h + squared_relu_mlp(rms_norm(h, mlp_norm_w[layer]), w_mlp_up[layer], w_mlp_down[layer])
    return rms_norm(h[:, N_META:], final_norm_w)
```

```python
import numpy as np
from contextlib import ExitStack
import concourse.bass as bass
import concourse.mybir as mybir
from concourse.bass_utils import run_bass_kernel_spmd

F32 = mybir.dt.float32
BF16 = mybir.dt.bfloat16
AF = mybir.ActivationFunctionType
ALU = mybir.AluOpType
AX = mybir.AxisListType

NCORES = 8
D = 1024
SEQ = 4096
NMETA = 16
DEPTH = 4
INC = 9248
DFF = 4096
KC = 8
EPS = 1e-6
TB = 512
NS = 160
EP = 30000
EPD = 1800


class Buf:
    __slots__ = ("name", "w", "r", "excl")

    def __init__(self, name, excl=False):
        self.name = name
        self.w = None
        self.r = []
        self.excl = excl


class Tok:
    __slots__ = ("kind", "eng", "n", "snapc", "snapd")


ENGS = ("pe", "act", "dve", "pool", "sp")
EIDX = {e: i for i, e in enumerate(ENGS)}


class Sched:
    def __init__(self):
        self.q = {e: [] for e in ENGS}
        self.cnt = [0] * 5
        self.seenc = {e: [0] * 5 for e in ENGS}
        self.seend = {e: {} for e in ENGS}
        self.dcnt = {}
        self.nops = 0
        self._defer = None

    def begin_defer(self):
        self._defer = []

    def end_defer(self):
        d = self._defer
        self._defer = None
        return d

    def _real(self, it):
        if it[0] == "add":
            self.add(*it[1:])
        else:
            self.dma(*it[1:])

    def replay_interleaved(self, A, B):
        nA = sum(1 for it in A if it[0] == "add" and it[1] == "pe")
        nB = sum(1 for it in B if it[0] == "add" and it[1] == "pe")
        iB = 0
        credit = 0.0
        for it in A:
            self._real(it)
            if it[0] == "add" and it[1] == "pe":
                credit += nB / max(nA, 1)
                while credit >= 1.0 and iB < len(B):
                    while iB < len(B):
                        jt = B[iB]
                        iB += 1
                        self._real(jt)
                        if jt[0] == "add" and jt[1] == "pe":
                            break
                    credit -= 1.0
        while iB < len(B):
            self._real(B[iB])
            iB += 1

    def _deps(self, eng, r, w):
        deps = []
        for b in r:
            if b.w is not None:
                deps.append(b.w)
            if b.excl:
                for t in b.r:
                    if t.kind == "d" or t.eng != eng:
                        deps.append(t)
        for b in w:
            if b.w is not None:
                deps.append(b.w)
            for t in b.r:
                if t.kind == "d" or t.eng != eng:
                    deps.append(t)
        return deps

    def _waits(self, eng, deps):
        sc = self.seenc[eng]
        sd = self.seend[eng]
        waits = []
        for t in deps:
            if t.kind == "c":
                ei = EIDX[t.eng]
                if sc[ei] >= t.n:
                    continue
                if t.eng == eng and eng == "pe":
                    continue
                waits.append(("c", t.eng, t.n))
            else:
                if sd.get(t.eng, 0) >= t.n:
                    continue
                waits.append(("d", t.eng, t.n))
            for i in range(5):
                if t.snapc[i] > sc[i]:
                    sc[i] = t.snapc[i]
            if t.kind == "c":
                ei = EIDX[t.eng]
                if t.n > sc[ei]:
                    sc[ei] = t.n
            if t.snapd is not sd:
                new = None
                for k, v in t.snapd.items():
                    if sd.get(k, 0) < v:
                        if new is None:
                            new = dict(sd)
                        new[k] = v
                        sd = new
                if new is not None:
                    self.seend[eng] = sd
            if t.kind == "d":
                if sd.get(t.eng, 0) < t.n:
                    sd = dict(sd)
                    sd[t.eng] = t.n
                    self.seend[eng] = sd
        best = {}
        for k, e, n in waits:
            if best.get((k, e), 0) < n:
                best[(k, e)] = n
        return [(k, e, n) for (k, e), n in best.items()]

    def add(self, eng, fn, r=(), w=(), attach=True):
        if self._defer is not None:
            self._defer.append(("add", eng, fn, tuple(r), tuple(w), attach))
            return None
        deps = self._deps(eng, r, w)
        waits = self._waits(eng, deps)
        ei = EIDX[eng]
        self.cnt[ei] += 1
        n = self.cnt[ei]
        t = Tok()
        t.kind = "c"
        t.eng = eng
        t.n = n
        t.snapc = tuple(self.seenc[eng])
        t.snapd = self.seend[eng]
        self.q[eng].append((waits, fn, ("c", eng, n), attach and eng != "pe"))
        for b in r:
            b.r.append(t)
        for b in w:
            b.w = t
            b.r = []
        self.nops += 1
        return t

    def dma(self, eng, fn, owner, r=(), w=(), nowaw=False):
        if self._defer is not None:
            self._defer.append(("dma", eng, fn, owner, tuple(r), tuple(w), nowaw))
            return None
        deps = self._deps(eng, r, () if nowaw else w)
        if owner.name in self.dcnt:
            pass
        waits = self._waits(eng, deps)
        n = self.dcnt.get(owner.name, 0) + 1
        self.dcnt[owner.name] = n
        t = Tok()
        t.kind = "d"
        t.eng = owner.name
        t.n = n
        t.snapc = tuple(self.seenc[eng])
        t.snapd = self.seend[eng]
        self.q[eng].append((waits, fn, ("d", owner.name, n), False))
        for b in r:
            b.r.append(t)
        for b in w:
            b.w = t
            b.r = []
        return t

    def final_wait(self, eng, bufs):
        deps = []
        for b in bufs:
            if b.w is not None:
                deps.append(b.w)
            deps.extend(b.r)
        waits = self._waits(eng, deps)
        self.q[eng].append((waits, None, None, False))

    def semkeys(self):
        keys = set()
        for e in ENGS:
            for waits, fn, inc, _ in self.q[e]:
                for k, en, n in waits:
                    keys.add(self._key(k, en, n)[0])
                if inc is not None:
                    keys.add(self._key(*inc)[0])
        return sorted(keys)

    @staticmethod
    def _key(kind, en, n):
        if kind == "c":
            return ("c", en, (n - 1) // EP), (n - 1) % EP + 1
        return ("d", en, (n - 1) // EPD), ((n - 1) % EPD + 1) * 16

    def emit(self, eng, h, sems):
        for waits, fn, inc, attach in self.q[eng]:
            ws = [self._key(*w) for w in waits]
            if fn is None:
                for k, v in ws:
                    h.wait_ge(sems[k], v)
                continue
            if attach and ws:
                for k, v in ws[:-1]:
                    h.wait_ge(sems[k], v)
                ins = fn(h)
                k, v = ws[-1]
                ins._wait_ge(sems[k], v)
            else:
                for k, v in ws:
                    h.wait_ge(sems[k], v)
                ins = fn(h)
            if inc is not None:
                k, v = self._key(*inc)
                ins.then_inc(sems[k], 16 if inc[0] == "d" else 1)


def bc_last(ap, n):
    return ap.to_broadcast(list(ap.shape) + [n])


def bc_mid(ap, n):
    a = [list(x) for x in ap.ap]
    assert len(a) == 2
    return bass.AP(ap.tensor, ap.offset, [a[0], [0, n], a[1]])


def w_tiles():
    tl = []
    tl.append(("lr", "w_in", 2048, 16, 8))
    tl.append(("q", "w_in", 0, 512, 8))
    tl.append(("k", "w_in", 512, 512, 8))
    tl.append(("v0", "w_in", 1024, 512, 8))
    tl.append(("v1", "w_in", 1536, 512, 8))
    tl.append(("gz0", "w_in", 2064, 512, 8))
    tl.append(("gz1", "w_in", 2576, 512, 8))
    for j in range(6):
        tl.append((f"dqkv{j}", "w_in", 3088 + 512 * j, 512, 8))
    tl.append(("dz0", "w_in", 6160, 512, 8))
    tl.append(("dz1", "w_in", 6672, 512, 8))
    tl.append(("ba", "w_in", 7184, 16, 8))
    tl.append(("gg0", "w_in", 7200, 512, 8))
    tl.append(("gg1", "w_in", 7712, 512, 8))
    tl.append(("gd0", "w_in", 8224, 512, 8))
    tl.append(("gd1", "w_in", 8736, 512, 8))
    for j in range(2):
        tl.append((f"bg{j}", "w_branch_gla", 512 * j, 512, 8))
    for j in range(2):
        tl.append((f"bd{j}", "w_branch_dn", 512 * j, 512, 8))
    for j in range(2):
        tl.append((f"wo{j}", "w_out", 512 * j, 512, 8))
    for j in range(8):
        tl.append((f"up{j}", "w_mlp_up", 512 * j, 512, 8))
    for hh in range(2):
        for j in range(4):
            tl.append((f"dn{hh}_{j}", "w_mlp_down", 256 * j, 256, 16, 2048 * hh))
    return [t if len(t) == 6 else t + (0,) for t in tl]


WT = w_tiles()
WIDX = {t[0]: i for i, t in enumerate(WT)}
NWT = len(WT)


import os as _os
KSTOP = float(_os.environ.get("KSTOP", "99"))
KV = int(_os.environ.get("KV", "0"))


def build_program(depth=DEPTH, nblocks=9, dbg=False):
    nc = bass.Bass("TRN2", target_bir_lowering=False)
    S = Sched()
    es = ExitStack()

    nreal = (nblocks - 1) * TB
    x_d = nc.dram_tensor("x", [max(nreal, 128), D], F32, kind="ExternalInput").ap()
    meta_d = nc.dram_tensor("meta", [NMETA, D], F32, kind="ExternalInput").ap()
    wd = {}
    wd["w_in"] = nc.dram_tensor("w_in", [depth, D, INC], F32, kind="ExternalInput").ap()
    wd["w_branch_gla"] = nc.dram_tensor("w_branch_gla", [depth, D, D], F32, kind="ExternalInput").ap()
    wd["w_branch_dn"] = nc.dram_tensor("w_branch_dn", [depth, D, D], F32, kind="ExternalInput").ap()
    wd["w_out"] = nc.dram_tensor("w_out", [depth, D, D], F32, kind="ExternalInput").ap()
    wd["w_mlp_up"] = nc.dram_tensor("w_mlp_up", [depth, D, DFF], F32, kind="ExternalInput").ap()
    wd["w_mlp_down"] = nc.dram_tensor("w_mlp_down", [depth, DFF, D], F32, kind="ExternalInput").ap()
    small_d = nc.dram_tensor("small", [depth, 128, NS], F32, kind="ExternalInput").ap()
    gup_d = nc.dram_tensor("gup", [depth, 16, 512], F32, kind="ExternalInput").ap()
    cm_d = nc.dram_tensor("cmask", [128, 20, 128], F32, kind="ExternalInput").ap()
    wfin_d = nc.dram_tensor("wfin", [128, D], F32, kind="ExternalInput").ap()
    out_d = nc.dram_tensor("out", [max(nreal, 128), D], F32, kind="ExternalOutput").ap()
    scr_d = nc.dram_tensor("wscr", [depth, NWT, 128, 4096], BF16, kind="Internal").ap()
    sst_d = nc.dram_tensor("sstate", [depth, 2, 128, 1024], F32, kind="Internal").ap()

    def sb(name, shape, dt):
        return es.enter_context(nc.sbuf_tensor("s_" + name, shape, dt))

    ps_t = es.enter_context(nc.psum_tensor("ps", [128, 4096], F32))
    ps_bf = ps_t.bitcast(BF16)
    PSB = [Buf(f"psb{i}", excl=True) for i in range(8)]

    def psf(b, n=512, off=0):
        return ps_t[:, b * 512 + off: b * 512 + off + n]

    def psh(b, n=1024, off=0):
        return ps_bf[:, b * 1024 + off: b * 1024 + off + n]

    class PSA:
        p1 = 0
        p2 = 0
        singles = list(range(8))
        pairs = [0, 2, 4, 6]

        @classmethod
        def mode(cls, m):
            if m == "dn":
                cls.singles, cls.pairs = [0, 1, 2, 3, 4, 5], [0, 2, 4]
            elif m == "gla":
                cls.singles, cls.pairs = [6, 7], [6]
            elif m == "h0":
                cls.singles, cls.pairs = [0, 1, 2, 3], [0, 2]
            elif m == "h1":
                cls.singles, cls.pairs = [4, 5, 6, 7], [4, 6]
            else:
                cls.singles, cls.pairs = list(range(8)), [0, 2, 4, 6]

        @classmethod
        def get1(cls):
            cls.p1 += 1
            return cls.singles[cls.p1 % len(cls.singles)]

        @classmethod
        def get2(cls):
            cls.p2 += 1
            return cls.pairs[cls.p2 % len(cls.pairs)]

    cm = sb("cm", [128, 5, 128], F32)
    cmb = sb("cmb", [128, 20, 128], BF16)
    B_cm = Buf("cm")
    wfin = sb("wfin", [128, D], F32)
    B_wfin = Buf("wfin")
    small = sb("small", [128, depth, NS], F32)
    B_small = Buf("small")
    negb = sb("negb", [128, depth, 4], F32)
    negA = sb("negA", [128, depth, 8], F32)
    gupb = sb("gupb", [16, 512], BF16)
    epsc = sb("epsc", [128, 1], F32)
    onec = sb("onec", [128, 1], F32)
    B_c1 = Buf("c1")

    ident_f = cm[:, 0, :]
    ident_b = cmb[:, 0, :]
    U_b = cmb[:, 1, :]
    SL_b = cmb[:, 2, :]
    ones_b = cmb[:, 4, :]
    ones_f = cm[:, 4, :]

    hT = sb("hT", [128, 8, TB], F32)
    B_h = Buf("hT")
    uT = sb("uT", [128, 8, TB], BF16)
    B_u = Buf("uT")
    Sg = sb("Sg", [128, 4, 256], F32)
    Sd = sb("Sd", [128, 8, 128], F32)
    B_Sg = Buf("Sg")
    B_Sd2 = [Buf("Sd0"), Buf("Sd1")]
    B_Sgd = [Buf(f"Sgd{l}") for l in range(depth)]
    B_Sdd = [Buf(f"Sdd{l}") for l in range(depth)]
    Sgb = sb("Sgb", [128, 4, 256], BF16)
    Sdb = sb("Sdb", [128, 8, 128], BF16)
    B_Sgb = Buf("Sgb")
    B_Sdb2 = [Buf("Sdb0"), Buf("Sdb1")]
    halo = sb("halo", [128, depth, 24, 4], BF16)
    B_halo = [Buf(f"halo{l}") for l in range(depth)]

    NSLOT = 4
    wring = [sb(f"wr{i}", [128, 4096], BF16) for i in range(NSLOT)]
    B_wr = [Buf(f"wr{i}") for i in range(NSLOT)]
    NGRP = 6
    B_scr = [[Buf(f"scr{l}_{g}") for g in range(NGRP)] for l in range(depth)]
    GRP = lambda ti: min(ti // 7, NGRP - 1)

    oTt = sb("oT", [128, 8, TB], BF16)
    oT = [oTt, oTt]
    B_oTt = Buf("oT")
    B_oT = [B_oTt, B_oTt]
    rstd = sb("rstd", [128, TB], F32)
    B_rstd = Buf("rstd")
    big = sb("big", [128, 24 * TB], BF16)
    B_big = Buf("big")
    B_sqn = B_big
    B_qk = [Buf(f"qk{g}") for g in range(8)]
    sqn = big[:, 0:8 * TB].rearrange("p (c t) -> p c t", c=8)
    vT_ = sb("vT", [128, 4, 1024], BF16)
    B_vT = Buf("vT")
    gz = sb("gz", [128, 4, 1024], BF16)
    B_gz = Buf("gz")
    ED = sb("ED", [128, 1024], BF16)
    EDT = sb("EDT", [128, 1024], BF16)
    B_ED, B_EDT = Buf("ED"), Buf("EDT")
    t1 = sb("t1", [128, 1024], F32)
    B_t1 = Buf("t1")
    mb = sb("mb", [128, 8, 128], BF16)
    B_mb = Buf("mb")
    usb = sb("usb", [128, 1024], F32)
    B_usb = Buf("usb")
    qe = sb("qe", [128, 4, TB], BF16)
    ke = sb("ke", [128, 4, TB], BF16)
    kdT = t1.bitcast(BF16)[:, 0:4 * TB].rearrange("p (c t) -> p c t", c=4)
    B_qe, B_ke, B_kdT = Buf("qe"), Buf("ke"), Buf("kdT")
    fa = ED.bitcast(F32)[:, 0:TB]
    fb = EDT.bitcast(F32)[:, 0:TB]
    B_fa, B_fb = Buf("fa"), Buf("fb")
    fc = sb("fc", [128, TB], F32)
    fd = sb("fd", [128, TB], F32)
    B_fc, B_fd = Buf("fc"), Buf("fd")
    def _h2(n):
        return [Buf(n + "0"), Buf(n + "1")]
    Bh = {n: _h2(n) for n in ("kbg", "kdd", "bv", "ED", "EDT", "t1", "mb", "Pm0", "Pm1", "Ptm0", "Ptm1", "Tt", "AqT",
                              "usb", "wTs", "vn", "ATm0", "ATm1")}

    xin = [usb, t1]
    yout = [usb, t1]
    B_xin_l = [Bh["usb"], Bh["t1"]]
    B_xin = [Bh["usb"][0], Bh["t1"][0]]
    B_yout = B_xin
    decg = sb("decg", [128, 4, 4], F32)
    B_decg = Buf("decg")
    glr = sb("glr", [16, TB], BF16)
    B_glr = Buf("glr")
    sc = sb("sc", [128, 512], BF16)
    B_sc = Buf("sc")
    kdt = sb("kdt", [128, 512], BF16)
    B_kdt = Buf("kdt")
    osq = sb("osq", [128, 1024], BF16)
    B_osq = Buf("osq")
    oss = sb("oss", [128, 8], F32)
    orr = sb("orr", [128, 8], F32)
    B_oss, B_orr = Buf("oss"), Buf("orr")
    onf = sb("onf", [128, 1024], F32)
    B_onf = Buf("onf")
    onz = sb("onz", [128, 1024], BF16)
    B_onz = Buf("onz")
    xraw = [sb(f"xraw{i}", [128, TB + 4], BF16) for i in range(2)]
    B_xraw = [Buf(f"xraw{i}") for i in range(2)]
    dg = [sb(f"dg{i}", [128, 4, 128], BF16) for i in range(2)]
    B_dg = [Buf(f"dg{i}") for i in range(2)]
    ba = sb("ba", [128, 4, 16], F32)
    B_ba = Buf("ba")
    sca = sb("sca", [128, 16, 8], F32)
    B_sca = Buf("sca")
    lab = sb("lab", [128, 8], BF16)
    B_lab = Buf("lab")
    RU = sb("RU", [128, 8, 128], BF16)
    RSL = sb("RSL", [128, 8, 128], BF16)
    B_RU, B_RSL = Buf("RU"), Buf("RSL")
    kbg = sb("kbg", [128, 8, 128], BF16)
    kdd = sb("kdd", [128, 8, 128], BF16)
    bv = sb("bv", [128, 8, 128], BF16)
    B_kbg, B_kdd, B_bv = Buf("kbg"), Buf("kdd"), Buf("bv")
    Pm = [sb(f"Pm{i}", [128, 8, 128], BF16) for i in range(2)]
    Ptm = [sb(f"Ptm{i}", [128, 8, 128], BF16) for i in range(2)]
    B_Pm = [Buf(f"Pm{i}") for i in range(2)]
    B_Ptm = [Buf(f"Ptm{i}") for i in range(2)]
    Tt = sb("Tt", [128, 8, 128], BF16)
    B_Tt = Buf("Tt")
    ATm = [sb(f"ATm{i}", [128, 8, 128], BF16) for i in range(2)]
    AqT = sb("AqT", [128, 8, 128], BF16)
    B_AqT = Buf("AqT")
    wTs = sb("wTs", [128, 8, 128], BF16)
    B_wTs = Buf("wTs")
    vn = sb("vn", [128, 8, 128], BF16)
    B_vn = Buf("vn")

    qkvT = big[:, 0:24 * TB].rearrange("p (c t) -> p c t", c=24)
    h2T = big[:, 0:16 * TB].rearrange("p (c t) -> p c t", c=16)
    zs = vT_
    B_zs = B_vT
    mrg = gz[:, :, :].rearrange("p a b -> p (a b)").rearrange("p (c t) -> p c t", c=8)
    B_mrg = B_gz
    sg2 = zs[:, :, :].rearrange("p a b -> p (a b)").rearrange("p (c t) -> p c t", c=8)
    B_sg2 = B_zs

    sm = lambda l, a, b: small[:, l, a:b]
    taps = {}

    def tap(name, ap, bufs, cond=True):
        if not dbg or not cond or name in taps:
            return
        shape = [int(x) for x in ap.shape]
        dt = ap.dtype
        d = nc.dram_tensor("dbg_" + name, shape, dt, kind="ExternalOutput").ap()
        b = Buf("dbg_" + name)
        taps[name] = b
        S.dma("sp", lambda h: h.dma_start(out=d, in_=ap), b, r=list(bufs), w=[b])

    S.dma("sp", lambda h: h.dma_start(out=cm[:], in_=cm_d[:, 0:5, :]), B_cm, w=[B_cm])
    S.dma("sp", lambda h: h.dma_start(out=wfin[:], in_=wfin_d), B_wfin, w=[B_wfin])
    S.dma("sp", lambda h: h.dma_start(out=small[:], in_=small_d.rearrange("l p n -> p l n")), B_small, w=[B_small])
    B_gupb = Buf("gupb")
    B_cmb = Buf("cmb")
    S.dma("pool", lambda h: h.dma_start(out=cmb[:], in_=cm_d), B_cmb, w=[B_cmb])
    S.add("dve", lambda h: h.memset(epsc[:], EPS), w=[B_c1])
    S.add("dve", lambda h: h.memset(onec[:], 1.0), w=[B_c1])
    B_neg = Buf("neg")
    S.add("dve", lambda h: h.tensor_scalar(out=negb[:], in0=small[:, :, 16:20], scalar1=-1.0, scalar2=None, op0=ALU.mult),
          r=[B_small], w=[B_neg])
    S.add("act", lambda h: h.activation(out=negA[:], in_=small[:, :, 132:140], func=AF.Exp), r=[B_small], w=[B_neg])
    S.add("dve", lambda h: h.tensor_scalar(out=negA[:], in0=negA[:], scalar1=-1.0, scalar2=None, op0=ALU.mult),
          r=[B_neg], w=[B_neg])
    S.add("dve", lambda h: h.memset(halo[:], 0.0), w=B_halo)

    def convert_tile(l, ti):
        nm, src, c0, ncol, kc, r0 = WT[ti]

        def f(h):
            s_ap = wd[src][l, r0:r0 + 128 * kc, c0:c0 + ncol].rearrange("(kc p) c -> p kc c", p=128)
            d_ap = scr_d[l, ti, :, 0:kc * ncol].rearrange("p (kc c) -> p kc c", kc=kc)
            return h.dma_start(out=d_ap, in_=s_ap)
        S.dma("pool", f, B_scr[l][GRP(ti)], w=[B_scr[l][GRP(ti)]], nowaw=True)

    for ti in range(NWT):
        convert_tile(0, ti)
    CUR = {"blk": 0}

    class WR:
        nxt = 0

    def wload(l, nm):
        ti = WIDX[nm]
        _, src, c0, ncol, kc, _r0 = WT[ti]
        s = WR.nxt
        WR.nxt = (WR.nxt + 1) % NSLOT
        n = kc * ncol

        def f(h):
            return h.dma_start(out=wring[s][:, 0:n], in_=scr_d[l, ti, :, 0:n])
        S.dma("sp", f, B_wr[s], r=[B_scr[l][GRP(ti)]], w=[B_wr[s]])
        if CUR["blk"] == 0 and l + 1 < depth:
            convert_tile(l + 1, ti)
        return B_wr[s], wring[s][:, 0:n].rearrange("p (kc c) -> p kc c", kc=kc)

    def mm_group(items, r, w):
        def f(h):
            ins = None
            for (o, a, b, st, sp) in items:
                ins = h.matmul(o, a, b, start=st, stop=sp)
            return ins
        S.add("pe", f, r=r, w=w)

    def tr_group(items, r, w):
        def f(h):
            ins = None
            for (o, a, idn) in items:
                ins = h.transpose(o, a, idn)
            return ins
        S.add("pe", f, r=r, w=w)

    def proj_F(wb, wt, c_lo, m, inT, B_in, kcn, ntok, bank):
        items = []
        for k in range(kcn):
            items.append((psf(bank)[0:m, 0:ntok], wt[:, k, c_lo:c_lo + m], inT[:, k, 0:ntok], k == 0, k == kcn - 1))
        mm_group(items, r=[wb, B_in], w=[PSB[bank]])

    def proj_T(wb, wt, ncol, i, bank):
        items = []
        for k in range(KC):
            items.append((psf(bank)[:, 0:ncol], uT[:, k, 128 * i:128 * i + 128], wt[:, k, 0:ncol], k == 0, k == KC - 1))
        mm_group(items, r=[wb, B_u], w=[PSB[bank]])

    def norm_F(wcols, ntok):
        S.add("act", lambda h: h.activation(out=sqn[:, :, 0:ntok], in_=hT[:, :, 0:ntok], func=AF.Square),
              r=[B_h], w=[B_sqn])
        bank = PSA.get1()
        mm_group([(psf(bank)[:, 0:ntok], ones_b, sqn[:, k, 0:ntok], k == 0, k == 7) for k in range(8)],
                 r=[B_sqn, B_cmb], w=[PSB[bank]])
        S.add("act", lambda h: h.activation(out=rstd[:, 0:ntok], in_=psf(bank)[:, 0:ntok], func=AF.Ln,
                                            bias=epsc[:, 0:1], scale=1.0 / D),
              r=[PSB[bank], B_c1], w=[B_rstd])
        S.add("act", lambda h: h.activation(out=rstd[:, 0:ntok], in_=rstd[:, 0:ntok], func=AF.Exp, scale=-0.5),
              r=[B_rstd], w=[B_rstd])
        for k in range(8):
            S.add("dve", lambda h, k=k: h.scalar_tensor_tensor(
                out=uT[:, k, 0:ntok], in0=hT[:, k, 0:ntok], scalar=wcols[:, k:k + 1],
                in1=rstd[:, 0:ntok], op0=ALU.mult, op1=ALU.mult),
                r=[B_h, B_rstd, B_small], w=[B_u])

    B_osq_h = [Buf("osq0"), Buf("osq1")]
    B_oss_h = [Buf("oss0"), Buf("oss1")]
    B_orr_h = [Buf("orr0"), Buf("orr1")]
    B_onf_h = [Buf("onf0"), Buf("onf1")]
    B_onz_h = [Buf("onz0"), Buf("onz1")]

    def out_path(o_ap, B_o, H, dv, gate_ap, B_gate, nwcols, which, i, half=None):
        if half is None:
            W, off, nch, cb, hb = 1024, 0, 8, 0, 0
            bs = lambda L: list(L)
        else:
            W, off, nch, cb, hb = 512, 512 * half, 4, 4 * half, 4 * half
            bs = lambda L: [L[half]]
        sq_ = osq[:, off:off + W]
        on_ = onf[:, off:off + W]
        oz_ = onz[:, off:off + W]
        ss_ = oss[:, hb:hb + H]
        rr_ = orr[:, hb:hb + H]
        S.add("act", lambda h: h.activation(out=sq_, in_=o_ap, func=AF.Square), r=B_o, w=bs(B_osq_h))
        S.add("dve", lambda h: h.tensor_reduce(out=ss_, in_=sq_.rearrange("p (h v) -> p h v", h=H),
                                               axis=AX.X, op=ALU.add), r=bs(B_osq_h), w=bs(B_oss_h))
        S.add("act", lambda h: h.activation(out=rr_, in_=ss_, func=AF.Ln, bias=epsc[:, 0:1],
                                            scale=1.0 / dv), r=bs(B_oss_h) + [B_c1], w=bs(B_orr_h))
        S.add("act", lambda h: h.activation(out=rr_, in_=rr_, func=AF.Exp, scale=-0.5),
              r=bs(B_orr_h), w=bs(B_orr_h))
        S.add("dve", lambda h: h.tensor_tensor(out=on_.rearrange("p (h v) -> p h v", h=H),
                                               in0=o_ap.rearrange("p (h v) -> p h v", h=H),
                                               in1=bc_last(rr_, dv), op=ALU.mult),
              r=list(B_o) + bs(B_orr_h), w=bs(B_onf_h))
        S.add("pool", lambda h: h.tensor_tensor(out=oz_, in0=on_, in1=gate_ap, op=ALU.mult),
              r=bs(B_onf_h) + [B_gate], w=bs(B_onz_h))
        bank = PSA.get1()
        tr_group([(psh(bank)[:, j * 128:(j + 1) * 128], oz_[:, j * 128:(j + 1) * 128], ident_b) for j in range(nch)],
                 r=bs(B_onz_h) + [B_cmb], w=[PSB[bank]])
        S.add("dve", lambda h: h.tensor_tensor(out=oT[which][:, cb:cb + nch, 128 * i:128 * i + 128],
                                               in0=psh(bank)[:, 0:nch * 128].rearrange("p (c t) -> p c t", c=nch),
                                               in1=bc_last(nwcols, 128), op=ALU.mult),
              r=[PSB[bank], B_small], w=[B_oT[which]])

    def gla_front(l, ntok, nt):
        dk = 128
        S.dma("pool", lambda h: h.dma_start(out=gupb[:], in_=gup_d[l]), B_gupb, w=[B_gupb])
        wb, wt = wload(l, "lr")
        tap("wlr", wt, [wb], nt == 4)
        bank = PSA.get1()
        proj_F(wb, wt, 0, 16, uT, B_u, KC, ntok, bank)
        S.add("act", lambda h, bank=bank: h.activation(out=glr[:, 0:ntok], in_=psf(bank)[0:16, 0:ntok], func=AF.Copy),
              r=[PSB[bank]], w=[B_glr])
        tap("glr", glr[:, 0:ntok], [B_glr], nt == 4)
        wbq, wq = wload(l, "q")
        wbk, wk = wload(l, "k")
        for hd in range(4):
            bz = PSA.get1()
            mm_group([(psf(bz)[:, 0:ntok], gupb[0:16, hd * 128:(hd + 1) * 128], glr[0:16, 0:ntok], True, True)],
                     r=[B_gupb, B_glr], w=[PSB[bz]])
            S.add("act", lambda h, hd=hd, bz=bz: h.activation(out=fa[:, 0:ntok], in_=psf(bz)[:, 0:ntok], func=AF.Exp,
                                                              bias=negb[:, l, hd:hd + 1], scale=-1.0),
                  r=[PSB[bz], B_neg], w=[B_fa])
            S.add("act", lambda h: h.activation(out=fa[:, 0:ntok], in_=fa[:, 0:ntok], func=AF.Ln,
                                                bias=onec[:, 0:1], scale=1.0), r=[B_fa, B_c1], w=[B_fa])
            S.add("dve", lambda h: h.tensor_scalar(out=fb[:, 0:ntok], in0=fa[:, 0:ntok], scalar1=-1.0 / 16.0,
                                                   scalar2=-1.0, op0=ALU.mult, op1=ALU.max), r=[B_fa], w=[B_fb])
            tap("la", fb[:, 0:ntok], [B_fb], hd == 3 and nt == 4)
            for i in range(nt):
                S.add("dve", lambda h, i=i: h.tensor_tensor_scan(
                    out=fc[:, 128 * i:128 * i + 128], data0=ones_f, data1=fb[:, 128 * i:128 * i + 128],
                    initial=0.0, op0=ALU.mult, op1=ALU.add), r=[B_fb, B_cm], w=[B_fc])
            S.add("act", lambda h: h.activation(out=fa[:, 0:ntok], in_=fc[:, 0:ntok], func=AF.Exp), r=[B_fc], w=[B_fa])
            S.add("act", lambda h: h.activation(out=fd[:, 0:ntok], in_=fc[:, 0:ntok], func=AF.Exp, scale=-1.0),
                  r=[B_fc], w=[B_fd])
            S.add("dve", lambda h, hd=hd: h.tensor_copy(
                out=decg[:, hd, 0:nt], in_=fa[:, 0:ntok].rearrange("p (i t) -> p i t", t=128)[:, :, 127]),
                r=[B_fa], w=[B_decg])
            tap("cum", fc[:, 0:ntok], [B_fc], hd == 3 and nt == 4)
            tap("E1", fa[:, 0:ntok], [B_fa], hd == 3 and nt == 4)
            tap("decg", decg[:], [B_decg], hd == 3 and nt == 4)
            bq = PSA.get1()
            proj_F(wbq, wq, hd * 128, 128, uT, B_u, KC, ntok, bq)
            S.add("dve", lambda h, hd=hd, bq=bq: h.scalar_tensor_tensor(
                out=qe[:, hd, 0:ntok], in0=psf(bq)[:, 0:ntok], scalar=float(dk) ** -0.5, in1=fa[:, 0:ntok],
                op0=ALU.mult, op1=ALU.mult), r=[PSB[bq], B_fa], w=[B_qe])
            bk = PSA.get1()
            proj_F(wbk, wk, hd * 128, 128, uT, B_u, KC, ntok, bk)
            S.add("dve", lambda h, hd=hd, bk=bk: h.tensor_tensor(
                out=ke[:, hd, 0:ntok], in0=psf(bk)[:, 0:ntok], in1=fd[:, 0:ntok], op=ALU.mult),
                r=[PSB[bk], B_fd], w=[B_ke])
            for i in range(nt):
                S.add("pool", lambda h, hd=hd, i=i: h.tensor_scalar(
                    out=kdT[:, hd, 128 * i:128 * i + 128], in0=ke[:, hd, 128 * i:128 * i + 128],
                    scalar1=decg[:, hd, i:i + 1], scalar2=1.0, op0=ALU.mult, op1=ALU.mult),
                    r=[B_ke, B_decg], w=[B_kdT])
        tap("qe", qe[:, :, 0:ntok], [B_qe], nt == 4)
        tap("ke", ke[:, :, 0:ntok], [B_ke], nt == 4)
        tap("kdT", kdT[:, :, 0:ntok], [B_kdT], nt == 4)
        tap("uT", uT[:], [B_u], nt == 4)
        for g in range(2):
            wb, wt = wload(l, f"v{g}")
            for i in range(nt):
                bank = PSA.get1()
                proj_T(wb, wt, 512, i, bank)
                S.add("act", lambda h, i=i, g=g, bank=bank: h.activation(
                    out=vT_[:, i, g * 512:(g + 1) * 512], in_=psf(bank), func=AF.Copy), r=[PSB[bank]], w=[B_vT])
        for g in range(2):
            wb, wt = wload(l, f"gz{g}")
            for i in range(nt):
                bank = PSA.get1()
                proj_T(wb, wt, 512, i, bank)
                S.add("act", lambda h, i=i, g=g, bank=bank: h.activation(
                    out=gz[:, i, g * 512:(g + 1) * 512], in_=psf(bank), func=AF.Silu), r=[PSB[bank]], w=[B_gz])
    def gla_loop(l, ntok, nt, first):
        if first:
            S.add("dve", lambda h: h.memset(Sg[:], 0.0), w=[B_Sg])
        else:
            S.dma("sp", lambda h: h.dma_start(out=Sg[:].rearrange("p a b -> p (a b)"), in_=sst_d[l, 0]), B_Sg,
                  r=[B_Sgd[l]], w=[B_Sg])
        S.add("act", lambda h: h.activation(out=Sgb[:], in_=Sg[:], func=AF.Copy), r=[B_Sg], w=[B_Sgb])
        for i in range(nt):
            tsl = slice(128 * i, 128 * i + 128)
            b1 = PSA.get1()
            mm_group([(psf(b1)[:, hd * 128:(hd + 1) * 128], ke[:, hd, tsl], qe[:, hd, tsl], True, True) for hd in range(4)],
                     r=[B_ke, B_qe], w=[PSB[b1]])
            S.add("dve", lambda h, b1=b1: h.tensor_tensor(
                out=sc[:].rearrange("p (a b) -> p a b", a=4), in0=psf(b1).rearrange("p (a b) -> p a b", a=4),
                in1=bc_mid(cm[:, 1, :], 4), op=ALU.mult), r=[PSB[b1], B_cm], w=[B_sc])
            b2 = PSA.get1()
            tr_group([(psh(b2)[:, hd * 128:(hd + 1) * 128], kdT[:, hd, tsl], ident_b) for hd in range(4)],
                     r=[B_kdT, B_cmb], w=[PSB[b2]])
            S.add("act", lambda h, b2=b2: h.activation(out=kdt[:], in_=psh(b2)[:, 0:512], func=AF.Copy),
                  r=[PSB[b2]], w=[B_kdt])
            b3 = PSA.get2()
            items = []
            for hd in range(4):
                o = ps_t[:, b3 * 512 + hd * 256: b3 * 512 + (hd + 1) * 256]
                items.append((o, sc[:, hd * 128:(hd + 1) * 128], vT_[:, i, hd * 256:(hd + 1) * 256], True, False))
                items.append((o, qe[:, hd, tsl], Sgb[:, hd, :], False, True))
            mm_group(items, r=[B_sc, B_vT, B_qe, B_Sgb], w=[PSB[b3], PSB[b3 + 1]])
            b4 = PSA.get2()
            mm_group([(ps_t[:, b4 * 512 + hd * 256: b4 * 512 + (hd + 1) * 256], kdt[:, hd * 128:(hd + 1) * 128],
                       vT_[:, i, hd * 256:(hd + 1) * 256], True, True) for hd in range(4)],
                     r=[B_kdt, B_vT], w=[PSB[b4], PSB[b4 + 1]])
            for hd in range(4):
                S.add("dve", lambda h, hd=hd, b4=b4, i=i: h.scalar_tensor_tensor(
                    out=Sg[:, hd, :], in0=Sg[:, hd, :], scalar=decg[:, hd, i:i + 1],
                    in1=ps_t[:, b4 * 512 + hd * 256: b4 * 512 + (hd + 1) * 256], op0=ALU.mult, op1=ALU.add),
                    r=[B_Sg, B_decg, PSB[b4], PSB[b4 + 1]], w=[B_Sg])
            if i < nt - 1:
                S.add("act", lambda h: h.activation(out=Sgb[:], in_=Sg[:], func=AF.Copy), r=[B_Sg], w=[B_Sgb])
            out_path(ps_t[:, b3 * 512:b3 * 512 + 1024], [PSB[b3], PSB[b3 + 1]], 4, 256,
                     gz[:, i, :], B_gz, sm(l, 20, 28), 0, i)

        S.dma("sp", lambda h: h.dma_start(out=sst_d[l, 0], in_=Sg[:].rearrange("p a b -> p (a b)")), B_Sg,
              r=[B_Sg], w=[B_Sgd[l]])

    def dn_conv(l, ntok, nt):
        st = {"wb": None, "wt": None}
        b1 = {}

        def stage1(c):
            if c % 4 == 0:
                st["wb"], st["wt"] = wload(l, f"dqkv{c // 4}")
            bank = PSA.get1()
            proj_F(st["wb"], st["wt"], (c % 4) * 128, 128, uT, B_u, KC, ntok, bank)
            xs = c % 2
            S.add("act", lambda h: h.activation(out=xraw[xs][:, 3:3 + ntok], in_=psf(bank)[:, 0:ntok],
                                                func=AF.Copy), r=[PSB[bank]], w=[B_xraw[xs]])
            S.add("pool", lambda h: h.tensor_copy(out=xraw[xs][:, 0:3], in_=halo[:, l, c, 0:3]),
                  r=[B_halo[l]], w=[B_xraw[xs]])
            S.add("pool", lambda h: h.tensor_copy(out=halo[:, l, c, 0:3], in_=xraw[xs][:, ntok:ntok + 3]),
                  r=[B_xraw[xs]], w=[B_halo[l]])
            S.add("dve", lambda h: h.tensor_tensor(
                out=dg[xs][:], in0=bc_mid(cm[:, 0, :], 4), in1=bc_last(small[:, l, 36 + 4 * c:40 + 4 * c], 128),
                op=ALU.mult), r=[B_cm, B_small], w=[B_dg[xs]])

        def stage2(c):
            xs = c % 2
            b2 = PSA.get1()
            mm_group([(psf(b2)[:, 0:ntok], dg[xs][:, j, :], xraw[xs][:, j:j + ntok], j == 0, j == 3) for j in range(4)],
                     r=[B_dg[xs], B_xraw[xs]], w=[PSB[b2]])
            S.add("act", lambda h: h.activation(out=qkvT[:, c, 0:ntok], in_=psf(b2)[:, 0:ntok], func=AF.Silu),
                  r=[PSB[b2]], w=[B_big])

        stage1(0)
        for c in range(24):
            if c + 1 < 24:
                stage1(c + 1)
            stage2(c)

    def dn_front(l, ntok, nt):
        dk = 128
        if KSTOP < 2.2:
            return
        for g in range(2):
            wb, wt = wload(l, f"dz{g}")
            for i in range(nt):
                bank = PSA.get1()
                proj_T(wb, wt, 512, i, bank)
                S.add("act", lambda h, i=i, g=g, bank=bank: h.activation(
                    out=zs[:, i, g * 512:(g + 1) * 512], in_=psf(bank), func=AF.Silu), r=[PSB[bank]], w=[B_zs])
        wb, wt = wload(l, "ba")
        for i in range(nt):
            bank = PSA.get1()
            proj_T(wb, wt, 16, i, bank)
            S.add("act", lambda h, i=i, bank=bank: h.activation(out=ba[:, i, :], in_=psf(bank)[:, 0:16], func=AF.Copy),
                  r=[PSB[bank]], w=[B_ba])
        if KSTOP < 2.3:
            return
        sqs = [osq[:, :].rearrange("p (a t) -> p a t", a=2), onz[:, :].rearrange("p (a t) -> p a t", a=2)]
        B_sqs = [B_osq_h, B_onz_h]
        rss = [usb[:, :].rearrange("p (a t) -> p a t", a=2), t1[:, :].rearrange("p (a t) -> p a t", a=2)]
        B_rss = [Bh["usb"], Bh["t1"]]
        for gi, c in enumerate(range(0, 16, 2)):
            B_g = B_qk[gi]
            sq_, B_sq = sqs[gi % 2], B_sqs[gi % 2]
            rs_, B_rs = rss[gi % 2], B_rss[gi % 2]
            S.add("pool", lambda h, c=c, sq_=sq_: h.tensor_tensor(out=sq_[:, :, 0:ntok], in0=qkvT[:, c:c + 2, 0:ntok],
                                                                  in1=qkvT[:, c:c + 2, 0:ntok], op=ALU.mult),
                  r=[B_big, B_g], w=B_sq)
            bp = PSA.get2()
            mm_group([(ps_t[:, (bp + a) * 512:(bp + a) * 512 + ntok], ones_b, sq_[:, a, 0:ntok], True, True) for a in range(2)],
                     r=B_sq + [B_cmb], w=[PSB[bp], PSB[bp + 1]])
            pv = ps_t[:, bp * 512:bp * 512 + 1024].rearrange("p (a t) -> p a t", a=2)[:, :, 0:ntok]
            S.add("act", lambda h, pv=pv, rs_=rs_: h.activation(out=rs_[:, :, 0:ntok], in_=pv, func=AF.Ln,
                                                                bias=epsc[:, 0:1], scale=1.0),
                  r=[PSB[bp], PSB[bp + 1], B_c1], w=B_rs)
            S.add("act", lambda h, rs_=rs_: h.activation(out=rs_[:, :, 0:ntok], in_=rs_[:, :, 0:ntok], func=AF.Exp, scale=-0.5),
                  r=B_rs, w=B_rs)
            S.add("dve", lambda h, c=c, rs_=rs_: h.tensor_tensor(out=qkvT[:, c:c + 2, 0:ntok], in0=qkvT[:, c:c + 2, 0:ntok],
                                                                 in1=rs_[:, :, 0:ntok], op=ALU.mult), r=[B_g] + B_rs, w=[B_g])
    def dn_half(l, i, hf, tsl, R):
        h0 = 4 * hf
        HS = slice(h0, h0 + 4)
        FS = slice(512 * hf, 512 * hf + 512)
        fl = lambda t: t[:, HS, :].rearrange("p a b -> p (a b)")
        p3 = lambda b: psh(b)[:, 0:512].rearrange("p (a b) -> p a b", a=4)
        f3 = lambda b: psf(b).rearrange("p (a b) -> p a b", a=4)
        B = {k: v[hf] for k, v in Bh.items()}
        B_Sdh, B_Sdbh = B_Sd2[hf], B_Sdb2[hf]
        bk = PSA.get1()
        tr_group([(psh(bk)[:, j * 128:(j + 1) * 128], qkvT[:, 8 + h0 + j, tsl], ident_b) for j in range(4)],
                 r=[B_big, B_cmb] + B_qk, w=[PSB[bk]])
        bvb = PSA.get1()
        tr_group([(psh(bvb)[:, j * 128:(j + 1) * 128], qkvT[:, 16 + h0 + j, tsl], ident_b) for j in range(4)],
                 r=[B_big, B_cmb], w=[PSB[bvb]])
        S.add("dve", lambda h: h.tensor_tensor(out=kbg[:, HS, :], in0=p3(bk), in1=bc_last(R(5)[:, HS], 128), op=ALU.mult),
              r=[PSB[bk], B_sca], w=[B["kbg"]])
        S.add("dve", lambda h: h.tensor_tensor(out=kdd[:, HS, :], in0=p3(bk), in1=bc_last(R(3)[:, HS], 128), op=ALU.mult),
              r=[PSB[bk], B_sca], w=[B["kdd"]])
        S.add("dve", lambda h: h.tensor_tensor(out=bv[:, HS, :], in0=p3(bvb), in1=bc_last(R(0)[:, HS], 128), op=ALU.mult),
              r=[PSB[bvb], B_sca], w=[B["bv"]])
        bD = PSA.get1()
        mm_group([(psf(bD), U_b, fl(RSL), True, True)], r=[B_cmb, B_RSL], w=[PSB[bD]])
        S.add("act", lambda h: h.activation(out=ED[:, FS], in_=psf(bD), func=AF.Exp), r=[PSB[bD]], w=[B["ED"]])
        bDT = PSA.get1()
        mm_group([(psf(bDT), SL_b, fl(RU), True, True)], r=[B_cmb, B_RU], w=[PSB[bDT]])
        S.add("act", lambda h: h.activation(out=EDT[:, FS], in_=psf(bDT), func=AF.Exp), r=[PSB[bDT]], w=[B["EDT"]])
        bG = PSA.get1()
        mm_group([(psf(bG)[:, j * 128:(j + 1) * 128], qkvT[:, 8 + h0 + j, tsl], qkvT[:, 8 + h0 + j, tsl], True, True)
                  for j in range(4)], r=[B_big] + B_qk, w=[PSB[bG]])
        S.add("dve", lambda h: h.tensor_tensor(out=t1[:, FS], in0=psf(bG), in1=ED[:, FS], op=ALU.mult),
              r=[PSB[bG], B["ED"]], w=[B["t1"]])
        S.add("pool", lambda h: h.tensor_tensor(out=mb[:, HS, :], in0=bc_mid(cmb[:, 2, :], 4), in1=bc_last(R(0)[:, HS], 128),
                                                op=ALU.mult), r=[B_cmb, B_sca], w=[B["mb"]])
        S.add("pool", lambda h: h.tensor_tensor(out=fl(Pm[0]), in0=t1[:, FS], in1=fl(mb), op=ALU.mult),
              r=[B["t1"], B["mb"]], w=[B["Pm0"]])
        bQ = PSA.get1()
        mm_group([(psf(bQ)[:, j * 128:(j + 1) * 128], qkvT[:, 8 + h0 + j, tsl], qkvT[:, h0 + j, tsl], True, True)
                  for j in range(4)], r=[B_big] + B_qk, w=[PSB[bQ]])
        S.add("dve", lambda h: h.tensor_tensor(out=t1[:, FS], in0=psf(bQ), in1=EDT[:, FS], op=ALU.mult),
              r=[PSB[bQ], B["EDT"]], w=[B["t1"]])
        S.add("pool", lambda h: h.tensor_tensor(out=AqT[:, HS, :], in0=t1[:, FS].rearrange("p (a b) -> p a b", a=4),
                                                in1=bc_mid(cm[:, 3, :], 4), op=ALU.mult), r=[B["t1"], B_cm], w=[B["AqT"]])
        bT = PSA.get1()
        tr_group([(psh(bT)[:, j * 128:(j + 1) * 128], Pm[0][:, h0 + j, :], ident_b) for j in range(4)],
                 r=[B["Pm0"], B_cmb], w=[PSB[bT]])
        S.add("act", lambda h: h.activation(out=fl(Ptm[0]), in_=psh(bT)[:, 0:512], func=AF.Copy),
              r=[PSB[bT]], w=[B["Ptm0"]])
        Tm = Pm[1]
        M1 = Ptm[1]
        S.add("pool", lambda h: h.tensor_tensor(out=M1[:, HS, :], in0=Pm[0][:, HS, :], in1=bc_mid(cmb[:, 6, :], 4), op=ALU.mult),
              r=[B["Pm0"], B_cmb], w=[B["Ptm1"]])
        S.add("dve", lambda h: h.tensor_tensor(out=Tm[:, HS, :], in0=bc_mid(cmb[:, 0, :], 4), in1=M1[:, HS, :], op=ALU.subtract),
              r=[B["Ptm1"], B_cmb], w=[B["Pm1"]])
        S.add("pool", lambda h: h.tensor_tensor(out=vn[:, HS, :], in0=Ptm[0][:, HS, :], in1=bc_mid(cmb[:, 5, :], 4), op=ALU.mult),
              r=[B["Ptm0"], B_cmb], w=[B["vn"]])
        S.add("dve", lambda h: h.tensor_tensor(out=Tt[:, HS, :], in0=bc_mid(cmb[:, 0, :], 4), in1=vn[:, HS, :], op=ALU.subtract),
              r=[B["vn"], B_cmb], w=[B["Tt"]])
        for j in range(1, 7):
            ATl = ATm[j % 2]
            S.add("pool", lambda h, j=j, ATl=ATl: h.tensor_tensor(out=ATl[:, HS, :], in0=Ptm[0][:, HS, :],
                                                                  in1=bc_mid(cmb[:, 13 + j, :], 4), op=ALU.mult),
                  r=[B["Ptm0"], B_cmb], w=[Bh["ATm%d" % (j % 2)][hf]])
            bM = PSA.get1()
            mm_group([(psf(bM)[:, q * 128:(q + 1) * 128], ATl[:, h0 + q, :], Tm[:, h0 + q, :], True, True) for q in range(4)],
                     r=[Bh["ATm%d" % (j % 2)][hf], B["Pm1"]], w=[PSB[bM]])
            S.add("act", lambda h, bM=bM: h.activation(out=fl(M1), in_=psf(bM), func=AF.Copy),
                  r=[PSB[bM]], w=[B["Ptm1"]])
            if j < 6:
                bdT = PSA.get1()
                mm_group([(psf(bdT)[:, q * 128:(q + 1) * 128], Tt[:, h0 + q, :], M1[:, h0 + q, :], True, True) for q in range(4)],
                         r=[B["Tt"], B["Ptm1"]], w=[PSB[bdT]])
            bdTt = PSA.get1()
            mm_group([(psf(bdTt)[:, q * 128:(q + 1) * 128], M1[:, h0 + q, :], Tt[:, h0 + q, :], True, True) for q in range(4)],
                     r=[B["Tt"], B["Ptm1"]], w=[PSB[bdTt]])
            if j < 6:
                S.add("dve", lambda h, bdT=bdT: h.tensor_tensor(out=fl(Tm), in0=fl(Tm), in1=psf(bdT), op=ALU.subtract),
                      r=[PSB[bdT], B["Pm1"]], w=[B["Pm1"]])
            S.add("dve", lambda h, bdTt=bdTt: h.tensor_tensor(out=fl(Tt), in0=fl(Tt), in1=psf(bdTt), op=ALU.subtract),
                  r=[PSB[bdTt], B["Tt"]], w=[B["Tt"]])
        bU = PSA.get1()
        mm_group([(psf(bU)[:, q * 128:(q + 1) * 128], Tt[:, h0 + q, :], bv[:, h0 + q, :], True, True) for q in range(4)],
                 r=[B["Tt"], B["bv"]], w=[PSB[bU]])
        S.add("act", lambda h: h.activation(out=usb[:, FS], in_=psf(bU), func=AF.Copy), r=[PSB[bU]], w=[B["usb"]])
        bW = PSA.get1()
        mm_group([(psf(bW)[:, q * 128:(q + 1) * 128], kbg[:, h0 + q, :], Tt[:, h0 + q, :], True, True) for q in range(4)],
                 r=[B["Tt"], B["kbg"]], w=[PSB[bW]])
        S.add("act", lambda h: h.activation(out=fl(wTs), in_=psf(bW), func=AF.Copy), r=[PSB[bW]], w=[B["wTs"]])
        bS = PSA.get1()
        mm_group([(psf(bS)[:, q * 128:(q + 1) * 128], wTs[:, h0 + q, :], Sdb[:, h0 + q, :], True, True) for q in range(4)],
                 r=[B["wTs"], B_Sdbh], w=[PSB[bS]])
        S.add("dve", lambda h: h.tensor_tensor(out=fl(vn), in0=usb[:, FS], in1=psf(bS), op=ALU.subtract),
              r=[PSB[bS], B["usb"]], w=[B["vn"]])
        bA = PSA.get1()
        mm_group([(psf(bA)[:, q * 128:(q + 1) * 128], qkvT[:, h0 + q, tsl], Sdb[:, h0 + q, :], True, True) for q in range(4)],
                 r=[B_big, B_Sdbh] + B_qk, w=[PSB[bA]])
        S.add("dve", lambda h: h.tensor_tensor(out=t1[:, FS].rearrange("p (a b) -> p a b", a=4), in0=f3(bA),
                                               in1=bc_last(R(8)[:, HS], 128), op=ALU.mult),
              r=[PSB[bA], B_sca], w=[B["t1"]])
        bB = PSA.get1()
        mm_group([(psf(bB)[:, q * 128:(q + 1) * 128], AqT[:, h0 + q, :], vn[:, h0 + q, :], True, True) for q in range(4)],
                 r=[B["AqT"], B["vn"]], w=[PSB[bB]])
        S.add("dve", lambda h: h.tensor_tensor(out=usb[:, FS], in0=psf(bB), in1=t1[:, FS], op=ALU.add),
              r=[PSB[bB], B["t1"]], w=[B["usb"]])
        bdS = PSA.get1()
        mm_group([(psf(bdS)[:, q * 128:(q + 1) * 128], kdd[:, h0 + q, :], vn[:, h0 + q, :], True, True) for q in range(4)],
                 r=[B["kdd"], B["vn"]], w=[PSB[bdS]])
        S.add("pool", lambda h: h.tensor_tensor(out=Sd[:, HS, :], in0=Sd[:, HS, :], in1=bc_last(R(4)[:, HS], 128), op=ALU.mult),
              r=[B_Sdh, B_sca], w=[B_Sdh])
        S.add("dve", lambda h: h.tensor_tensor(out=fl(Sd), in0=psf(bdS), in1=fl(Sd), op=ALU.add),
              r=[PSB[bdS], B_Sdh], w=[B_Sdh])
        S.add("act", lambda h: h.activation(out=Sdb[:, HS, :], in_=Sd[:, HS, :], func=AF.Copy), r=[B_Sdh], w=[B_Sdbh])
        out_path(usb[:, FS], [B["usb"]], 4, 128, zs[:, i, FS], B_zs, small[:, l, 28 + h0:32 + h0], 1, i, half=hf)

    def dn_loop(l, ntok, nt, first):
        dk = 128
        if first:
            S.add("dve", lambda h: h.memset(Sd[:], 0.0), w=B_Sd2)
        else:
            S.dma("sp", lambda h: h.dma_start(out=Sd[:].rearrange("p a b -> p (a b)"), in_=sst_d[l, 1]), B_Sd2[0],
                  r=[B_Sdd[l]], w=B_Sd2)
        S.add("act", lambda h: h.activation(out=Sdb[:], in_=Sd[:], func=AF.Copy), r=B_Sd2, w=B_Sdb2)
        R = lambda k: sca[:, k, :]
        for i in range(nt):
            tsl = slice(128 * i, 128 * i + 128)
            S.add("act", lambda h, i=i: h.activation(out=R(0), in_=ba[:, i, 0:8], func=AF.Exp, scale=-1.0),
                  r=[B_ba], w=[B_sca])
            S.add("dve", lambda h: h.tensor_scalar(out=R(0), in0=R(0), scalar1=1.0, scalar2=None, op0=ALU.add),
                  r=[B_sca], w=[B_sca])
            S.add("dve", lambda h: h.reciprocal(out=R(0), in_=R(0)), r=[B_sca], w=[B_sca])
            S.add("dve", lambda h, i=i: h.tensor_tensor(out=R(1), in0=ba[:, i, 8:16], in1=small[:, l, 140:148], op=ALU.add),
                  r=[B_ba, B_small], w=[B_sca])
            S.add("act", lambda h: h.activation(out=R(1), in_=R(1), func=AF.Exp), r=[B_sca], w=[B_sca])
            S.add("act", lambda h: h.activation(out=R(1), in_=R(1), func=AF.Ln, bias=onec[:, 0:1], scale=1.0),
                  r=[B_sca, B_c1], w=[B_sca])
            S.add("dve", lambda h: h.tensor_tensor(out=lab[:], in0=R(1), in1=negA[:, l, :], op=ALU.mult),
                  r=[B_sca, B_neg], w=[B_lab])
            S.add("pool", lambda h: h.tensor_tensor(out=RU[:], in0=bc_mid(cmb[:, 1, :], 8), in1=bc_last(lab[:], 128),
                                                    op=ALU.mult), r=[B_cmb, B_lab], w=[B_RU])
            S.add("pool", lambda h: h.tensor_tensor(out=RSL[:], in0=bc_mid(cmb[:, 2, :], 8), in1=bc_last(lab[:], 128),
                                                    op=ALU.mult), r=[B_cmb, B_lab], w=[B_RSL])
            bg = PSA.get1()
            mm_group([(psf(bg)[:, 0:8], U_b, lab[:], True, True), (psf(bg)[:, 8:16], ones_b, lab[:], True, True)],
                     r=[B_cmb, B_lab], w=[PSB[bg]])
            S.add("act", lambda h, bg=bg: h.activation(out=sca[:, 6:8, :].rearrange("p a b -> p (a b)"),
                                                       in_=psf(bg)[:, 0:16], func=AF.Copy), r=[PSB[bg]], w=[B_sca])
            S.add("act", lambda h: h.activation(out=R(2), in_=R(6), func=AF.Exp), r=[B_sca], w=[B_sca])
            S.add("act", lambda h: h.activation(out=R(4), in_=R(7), func=AF.Exp), r=[B_sca], w=[B_sca])
            S.add("dve", lambda h: h.tensor_tensor(out=R(3), in0=R(7), in1=R(6), op=ALU.subtract), r=[B_sca], w=[B_sca])
            S.add("act", lambda h: h.activation(out=R(3), in_=R(3), func=AF.Exp), r=[B_sca], w=[B_sca])
            S.add("dve", lambda h: h.tensor_tensor(out=R(5), in0=R(0), in1=R(2), op=ALU.mult), r=[B_sca], w=[B_sca])
            S.add("dve", lambda h: h.tensor_scalar(out=R(8), in0=R(2), scalar1=float(dk) ** -0.5, scalar2=None,
                                                   op0=ALU.mult), r=[B_sca], w=[B_sca])
            halves = []
            for hf in range(2):
                S.begin_defer()
                PSA.mode("h%d" % hf)
                dn_half(l, i, hf, tsl, R)
                halves.append(S.end_defer())
            PSA.mode("all")
            (S.replay_interleaved(halves[0], halves[1]) if KV == 0 else [S._real(it) for hh in halves for it in hh])
        S.dma("sp", lambda h: h.dma_start(out=sst_d[l, 1], in_=Sd[:].rearrange("p a b -> p (a b)")), B_Sd2[0],
              r=B_Sd2, w=[B_Sdd[l]])

    def merge_gla(l, ntok):
        for g in range(2):
            wb, wt = wload(l, f"gg{g}")
            for c in range(4):
                bank = PSA.get1()
                proj_F(wb, wt, c * 128, 128, uT, B_u, KC, ntok, bank)
                S.add("act", lambda h, g=g, c=c, bank=bank: h.activation(
                    out=mrg[:, g * 4 + c, 0:ntok], in_=psf(bank)[:, 0:ntok], func=AF.Sigmoid),
                    r=[PSB[bank]], w=[B_mrg])
        for g in range(2):
            wb, wt = wload(l, f"bg{g}")
            for c in range(4):
                j = g * 4 + c
                bank = PSA.get1()
                proj_F(wb, wt, c * 128, 128, oT[0], B_oT[0], KC, ntok, bank)
                S.add("dve", lambda h, j=j, bank=bank: h.tensor_tensor(
                    out=mrg[:, j, 0:ntok], in0=psf(bank)[:, 0:ntok], in1=mrg[:, j, 0:ntok], op=ALU.mult),
                    r=[PSB[bank], B_mrg], w=[B_mrg])

    def merge_dn(l, ntok):
        for g in range(2):
            wb, wt = wload(l, f"gd{g}")
            for c in range(4):
                bank = PSA.get1()
                proj_F(wb, wt, c * 128, 128, uT, B_u, KC, ntok, bank)
                S.add("act", lambda h, g=g, c=c, bank=bank: h.activation(
                    out=sg2[:, g * 4 + c, 0:ntok], in_=psf(bank)[:, 0:ntok], func=AF.Sigmoid),
                    r=[PSB[bank]], w=[B_sg2])
        for g in range(2):
            wb, wt = wload(l, f"bd{g}")
            for c in range(4):
                j = g * 4 + c
                bank = PSA.get1()
                proj_F(wb, wt, c * 128, 128, oT[1], B_oT[1], KC, ntok, bank)
                S.add("dve", lambda h, j=j, bank=bank: h.tensor_tensor(
                    out=sg2[:, j, 0:ntok], in0=psf(bank)[:, 0:ntok], in1=sg2[:, j, 0:ntok], op=ALU.mult),
                    r=[PSB[bank], B_sg2], w=[B_sg2])

    def merge_phase(l, ntok):
        S.add("pool", lambda h: h.tensor_tensor(out=mrg[:, :, 0:ntok], in0=mrg[:, :, 0:ntok], in1=sg2[:, :, 0:ntok],
                                                op=ALU.add), r=[B_mrg, B_sg2], w=[B_mrg])
        for g in range(2):
            wb, wt = wload(l, f"wo{g}")
            for c in range(4):
                j = g * 4 + c
                bank = PSA.get1()
                proj_F(wb, wt, c * 128, 128, mrg, B_mrg, KC, ntok, bank)
                S.add("dve", lambda h, j=j, bank=bank: h.tensor_tensor(
                    out=hT[:, j, 0:ntok], in0=psf(bank)[:, 0:ntok], in1=hT[:, j, 0:ntok], op=ALU.add),
                    r=[PSB[bank], B_h], w=[B_h])

    def mlp_phase(l, ntok):
        norm_F(sm(l, 8, 16), ntok)
        for hh in range(2):
            for g in range(4):
                wb, wt = wload(l, f"up{hh * 4 + g}")
                for c in range(4):
                    j = g * 4 + c
                    bank = PSA.get1()
                    proj_F(wb, wt, c * 128, 128, uT, B_u, KC, ntok, bank)
                    S.add("act", lambda h, j=j, bank=bank: h.activation(out=h2T[:, j, 0:ntok], in_=psf(bank)[:, 0:ntok],
                                                                        func=AF.Relu), r=[PSB[bank]], w=[B_big])
                    S.add("pool", lambda h, j=j: h.tensor_tensor(out=h2T[:, j, 0:ntok], in0=h2T[:, j, 0:ntok],
                                                                 in1=h2T[:, j, 0:ntok], op=ALU.mult), r=[B_big], w=[B_big])
            for g in range(4):
                wb, wt = wload(l, f"dn{hh}_{g}")
                for c in range(2):
                    j = g * 2 + c
                    bank = PSA.get1()
                    proj_F(wb, wt, c * 128, 128, h2T, B_big, 16, ntok, bank)
                    S.add("dve", lambda h, j=j, bank=bank: h.tensor_tensor(
                        out=hT[:, j, 0:ntok], in0=psf(bank)[:, 0:ntok], in1=hT[:, j, 0:ntok], op=ALU.add),
                        r=[PSB[bank], B_h], w=[B_h])

    xcnt = [0]
    for blk in range(nblocks):
        CUR["blk"] = blk
        nt = 1 if blk == 0 else 4
        ntok = 128 * nt
        for i in range(nt):
            xs = xcnt[0] % 2
            xcnt[0] += 1
            if blk == 0:
                S.add("dve", lambda h, xs=xs: h.memset(xin[xs][:], 0.0), w=B_xin_l[xs])
                S.dma("sp", lambda h, xs=xs: h.dma_start(out=xin[xs][128 - NMETA:128, :], in_=meta_d), B_xin[xs],
                      w=B_xin_l[xs])
            else:
                r0 = (blk - 1) * TB + 128 * i
                S.dma("sp", lambda h, xs=xs, r0=r0: h.dma_start(out=xin[xs][:], in_=x_d[r0:r0 + 128, :]), B_xin[xs],
                      w=B_xin_l[xs])
            for half in range(2):
                bank = PSA.get1()
                tr_group([(psf(bank)[:, c * 128:(c + 1) * 128], xin[xs][:, (half * 4 + c) * 128:(half * 4 + c + 1) * 128],
                           ident_f) for c in range(4)], r=B_xin_l[xs] + [B_cm], w=[PSB[bank]])
                S.add("act", lambda h, half=half, i=i, bank=bank: h.activation(
                    out=hT[:, half * 4:half * 4 + 4, 128 * i:128 * i + 128],
                    in_=psf(bank).rearrange("p (c t) -> p c t", c=4), func=AF.Copy), r=[PSB[bank]], w=[B_h])
        STOP = KSTOP
        for l in range(depth):
            if STOP >= 1:
                norm_F(sm(l, 0, 8), ntok)
            gla_front(l, ntok, nt)
            S.begin_defer()
            PSA.mode("dn")
            gla_loop(l, ntok, nt, blk == 0)
            A_ops = S.end_defer()
            S.begin_defer()
            PSA.mode("gla")
            dn_conv(l, ntok, nt)
            B_ops = S.end_defer()
            PSA.mode("all")
            S.replay_interleaved(A_ops, B_ops)
            merge_gla(l, ntok)
            dn_front(l, ntok, nt)
            dn_loop(l, ntok, nt, blk == 0)
            merge_dn(l, ntok)
            if STOP >= 4:
                merge_phase(l, ntok)
            if STOP >= 5:
                mlp_phase(l, ntok)
        tap("hT", hT[:], [B_h], blk == nblocks - 1)
        if blk > 0:
            for i in range(nt):
                ys = xcnt[0] % 2
                xcnt[0] += 1
                bp = PSA.get2()
                tr_group([(ps_t[:, bp * 512 + c * 128: bp * 512 + (c + 1) * 128], hT[:, c, 128 * i:128 * i + 128], ident_f)
                          for c in range(8)], r=[B_h, B_cm], w=[PSB[bp], PSB[bp + 1]])
                S.add("act", lambda h, bp=bp: h.activation(out=onf[:], in_=ps_t[:, bp * 512:bp * 512 + 1024],
                                                           func=AF.Square, accum_out=oss[:, 0:1]),
                      r=[PSB[bp], PSB[bp + 1]], w=B_onf_h + B_oss_h, attach=False)
                S.add("act", lambda h: h.activation(out=orr[:, 0:1], in_=oss[:, 0:1], func=AF.Ln, bias=epsc[:, 0:1],
                                                    scale=1.0 / D), r=B_oss_h + [B_c1], w=B_orr_h)
                S.add("act", lambda h: h.activation(out=orr[:, 0:1], in_=orr[:, 0:1], func=AF.Exp, scale=-0.5),
                      r=B_orr_h, w=B_orr_h)
                S.add("dve", lambda h, bp=bp, ys=ys: h.scalar_tensor_tensor(
                    out=yout[ys][:], in0=ps_t[:, bp * 512:bp * 512 + 1024], scalar=orr[:, 0:1], in1=wfin[:],
                    op0=ALU.mult, op1=ALU.mult), r=[PSB[bp], PSB[bp + 1], B_wfin] + B_orr_h, w=B_xin_l[ys])
                r0 = (blk - 1) * TB + 128 * i
                S.dma("sp", lambda h, ys=ys, r0=r0: h.dma_start(out=out_d[r0:r0 + 128, :], in_=yout[ys][:]), B_yout[ys],
                      r=B_xin_l[ys])
    S.final_wait("sp", B_xin_l[0] + B_xin_l[1] + [B_Sg] + B_Sd2 + B_wr + [b for bl in B_scr for b in bl] + list(taps.values()))

    keys = S.semkeys()
    sems = {}
    for i, k in enumerate(keys):
        sems[k] = es.enter_context(nc.semaphore(f"s{i}"))
    with nc.Block() as block:
        @block.tensor
        def _(h):
            S.emit("pe", h, sems)

        @block.scalar
        def _(h):
            S.emit("act", h, sems)

        @block.vector
        def _(h):
            S.emit("dve", h, sems)

        @block.gpsimd
        def _(h):
            S.emit("pool", h, sems)

        @block.sync
        def _(h):
            S.emit("sp", h, sems)
    es.close()
    return nc, S


def make_consts():
    cm = np.zeros((128, 20, 128), np.float32)
    r = np.arange(128)[:, None]
    c = np.arange(128)[None, :]
    cm[:, 0] = (r == c)
    cm[:, 1] = (r <= c)
    cm[:, 2] = (r > c)
    cm[:, 3] = (r <= c) * np.float32(128.0 ** -0.5)
    cm[:, 4] = 1.0
    for j in range(7):
        b = 1 << j
        cm[:, 6 + j] = ((r // (2 * b)) == (c // (2 * b))) & ((r % (2 * b)) >= b) & ((c % (2 * b)) < b)
    cm[:, 5] = cm[:, 6].T
    for j in range(7):
        cm[:, 13 + j] = cm[:, 6 + j].T
    return cm


def pack_small(inp, depth):
    sm = np.zeros((depth, 128, NS), np.float32)
    for l in range(depth):
        sm[l, :, 0:8] = inp["mixer_norm_w"][l].reshape(8, 128).T
        sm[l, :, 8:16] = inp["mlp_norm_w"][l].reshape(8, 128).T
        sm[l, :, 16:20] = inp["gla_b_gate"][l].reshape(4, 128).T
        gn = inp["gla_norm_w"][l].reshape(2, 128).T
        sm[l, :, 20:28] = np.tile(gn, (1, 4))
        sm[l, :, 28:36] = np.tile(inp["dn_norm_w"][l].reshape(128, 1), (1, 8))
        cw = inp["dn_conv_w"][l]
        sm[l, :, 36:132] = cw.reshape(4, 24, 128).transpose(2, 1, 0).reshape(128, 96)
        sm[l, :, 132:140] = np.broadcast_to(inp["dn_a_log"][l][None, :], (128, 8))
        sm[l, :, 140:148] = np.broadcast_to(inp["dn_dt_bias"][l][None, :], (128, 8))
    return sm


_CACHE = {}


def run(inputs, depth=DEPTH, nblocks=9, ncores=NCORES, dbg=False, trace=False):
    key = (depth, nblocks, dbg)
    if key not in _CACHE:
        _CACHE[key] = build_program(depth, nblocks, dbg)
    nc, S = _CACHE[key]
    f = lambda a: np.ascontiguousarray(np.asarray(a, dtype=np.float32))
    inp = {k: np.asarray(v) for k, v in inputs.items()}
    nreal = (nblocks - 1) * TB
    shared = {
        "meta": f(inp["meta_tokens"]),
        "w_in": f(inp["w_in"][:depth]),
        "w_branch_gla": f(inp["w_branch_gla"][:depth]),
        "w_branch_dn": f(inp["w_branch_dn"][:depth]),
        "w_out": f(inp["w_out"][:depth]),
        "w_mlp_up": f(inp["w_mlp_up"][:depth]),
        "w_mlp_down": f(inp["w_mlp_down"][:depth]),
        "small": pack_small(inp, depth),
        "gup": f(inp["gla_w_gate_up"][:depth]),
        "cmask": make_consts(),
        "wfin": f(np.broadcast_to(inp["final_norm_w"][None, :], (128, D))),
    }
    in_maps = []
    for c in range(ncores):
        m = dict(shared)
        m["x"] = f(inp["x"][c, :max(nreal, 128)])
        in_maps.append(m)
    res = run_bass_kernel_spmd(nc, in_maps, core_ids=list(range(ncores)), trace=trace)
    return res


def kernel(**inputs):
    res = run(inputs)
    out = np.stack([np.asarray(r["out"], dtype=np.float32) for r in res.results], axis=0)
    return out
```

```python
import numpy as np
from contextlib import ExitStack
import concourse.bass as bass
import concourse.mybir as mybir
from concourse.bass_utils import run_bass_kernel_spmd

F32 = mybir.dt.float32
BF16 = mybir.dt.bfloat16
AF = mybir.ActivationFunctionType
ALU = mybir.AluOpType
AX = mybir.AxisListType

NCORES = 8
D = 1024
SEQ = 4096
NMETA = 16
DEPTH = 4
INC = 9248
DFF = 4096
KC = 8
EPS = 1e-6
TB = 512
NS = 160
EP = 30000
EPD = 1800


class Buf:
    __slots__ = ("name", "w", "r", "excl")

    def __init__(self, name, excl=False):
        self.name = name
        self.w = None
        self.r = []
        self.excl = excl


class Tok:
    __slots__ = ("kind", "eng", "n", "snapc", "snapd")


ENGS = ("pe", "act", "dve", "pool", "sp")
EIDX = {e: i for i, e in enumerate(ENGS)}


class Sched:
    def __init__(self):
        self.q = {e: [] for e in ENGS}
        self.cnt = [0] * 5
        self.seenc = {e: [0] * 5 for e in ENGS}
        self.seend = {e: {} for e in ENGS}
        self.dcnt = {}
        self.nops = 0
        self._defer = None

    def begin_defer(self):
        self._defer = []

    def end_defer(self):
        d = self._defer
        self._defer = None
        return d

    def _real(self, it):
        if it[0] == "add":
            self.add(*it[1:])
        else:
            self.dma(*it[1:])

    def replay_interleaved(self, A, B):
        nA = sum(1 for it in A if it[0] == "add" and it[1] == "pe")
        nB = sum(1 for it in B if it[0] == "add" and it[1] == "pe")
        iB = 0
        credit = 0.0
        for it in A:
            self._real(it)
            if it[0] == "add" and it[1] == "pe":
                credit += nB / max(nA, 1)
                while credit >= 1.0 and iB < len(B):
                    while iB < len(B):
                        jt = B[iB]
                        iB += 1
                        self._real(jt)
                        if jt[0] == "add" and jt[1] == "pe":
                            break
                    credit -= 1.0
        while iB < len(B):
            self._real(B[iB])
            iB += 1

    def _deps(self, eng, r, w):
        deps = []
        for b in r:
            if b.w is not None:
                deps.append(b.w)
            if b.excl:
                for t in b.r:
                    if t.kind == "d" or t.eng != eng:
                        deps.append(t)
        for b in w:
            if b.w is not None:
                deps.append(b.w)
            for t in b.r:
                if t.kind == "d" or t.eng != eng:
                    deps.append(t)
        return deps

    def _waits(self, eng, deps):
        sc = self.seenc[eng]
        sd = self.seend[eng]
        waits = []
        for t in deps:
            if t.kind == "c":
                ei = EIDX[t.eng]
                if sc[ei] >= t.n:
                    continue
                if t.eng == eng and eng == "pe":
                    continue
                waits.append(("c", t.eng, t.n))
            else:
                if sd.get(t.eng, 0) >= t.n:
                    continue
                waits.append(("d", t.eng, t.n))
            for i in range(5):
                if t.snapc[i] > sc[i]:
                    sc[i] = t.snapc[i]
            if t.kind == "c":
                ei = EIDX[t.eng]
                if t.n > sc[ei]:
                    sc[ei] = t.n
            if t.snapd is not sd:
                new = None
                for k, v in t.snapd.items():
                    if sd.get(k, 0) < v:
                        if new is None:
                            new = dict(sd)
                        new[k] = v
                        sd = new
                if new is not None:
                    self.seend[eng] = sd
            if t.kind == "d":
                if sd.get(t.eng, 0) < t.n:
                    sd = dict(sd)
                    sd[t.eng] = t.n
                    self.seend[eng] = sd
        best = {}
        for k, e, n in waits:
            if best.get((k, e), 0) < n:
                best[(k, e)] = n
        return [(k, e, n) for (k, e), n in best.items()]

    def add(self, eng, fn, r=(), w=(), attach=True):
        if self._defer is not None:
            self._defer.append(("add", eng, fn, tuple(r), tuple(w), attach))
            return None
        deps = self._deps(eng, r, w)
        waits = self._waits(eng, deps)
        ei = EIDX[eng]
        self.cnt[ei] += 1
        n = self.cnt[ei]
        t = Tok()
        t.kind = "c"
        t.eng = eng
        t.n = n
        t.snapc = tuple(self.seenc[eng])
        t.snapd = self.seend[eng]
        self.q[eng].append((waits, fn, ("c", eng, n), attach and eng != "pe"))
        for b in r:
            b.r.append(t)
        for b in w:
            b.w = t
            b.r = []
        self.nops += 1
        return t

    def dma(self, eng, fn, owner, r=(), w=(), nowaw=False):
        if self._defer is not None:
            self._defer.append(("dma", eng, fn, owner, tuple(r), tuple(w), nowaw))
            return None
        deps = self._deps(eng, r, () if nowaw else w)
        if owner.name in self.dcnt:
            pass
        waits = self._waits(eng, deps)
        n = self.dcnt.get(owner.name, 0) + 1
        self.dcnt[owner.name] = n
        t = Tok()
        t.kind = "d"
        t.eng = owner.name
        t.n = n
        t.snapc = tuple(self.seenc[eng])
        t.snapd = self.seend[eng]
        self.q[eng].append((waits, fn, ("d", owner.name, n), False))
        for b in r:
            b.r.append(t)
        for b in w:
            b.w = t
            b.r = []
        return t

    def final_wait(self, eng, bufs):
        deps = []
        for b in bufs:
            if b.w is not None:
                deps.append(b.w)
            deps.extend(b.r)
        waits = self._waits(eng, deps)
        self.q[eng].append((waits, None, None, False))

    def semkeys(self):
        keys = set()
        for e in ENGS:
            for waits, fn, inc, _ in self.q[e]:
                for k, en, n in waits:
                    keys.add(self._key(k, en, n)[0])
                if inc is not None:
                    keys.add(self._key(*inc)[0])
        return sorted(keys)

    @staticmethod
    def _key(kind, en, n):
        if kind == "c":
            return ("c", en, (n - 1) // EP), (n - 1) % EP + 1
        return ("d", en, (n - 1) // EPD), ((n - 1) % EPD + 1) * 16

    def emit(self, eng, h, sems):
        for waits, fn, inc, attach in self.q[eng]:
            ws = [self._key(*w) for w in waits]
            if fn is None:
                for k, v in ws:
                    h.wait_ge(sems[k], v)
                continue
            if attach and ws:
                for k, v in ws[:-1]:
                    h.wait_ge(sems[k], v)
                ins = fn(h)
                k, v = ws[-1]
                ins._wait_ge(sems[k], v)
            else:
                for k, v in ws:
                    h.wait_ge(sems[k], v)
                ins = fn(h)
            if inc is not None:
                k, v = self._key(*inc)
                ins.then_inc(sems[k], 16 if inc[0] == "d" else 1)


def bc_last(ap, n):
    return ap.to_broadcast(list(ap.shape) + [n])


def bc_mid(ap, n):
    a = [list(x) for x in ap.ap]
    assert len(a) == 2
    return bass.AP(ap.tensor, ap.offset, [a[0], [0, n], a[1]])


def w_tiles():
    tl = []
    tl.append(("lr", "w_in", 2048, 16, 8))
    tl.append(("q", "w_in", 0, 512, 8))
    tl.append(("k", "w_in", 512, 512, 8))
    tl.append(("v0", "w_in", 1024, 512, 8))
    tl.append(("v1", "w_in", 1536, 512, 8))
    tl.append(("gz0", "w_in", 2064, 512, 8))
    tl.append(("gz1", "w_in", 2576, 512, 8))
    for j in range(6):
        tl.append((f"dqkv{j}", "w_in", 3088 + 512 * j, 512, 8))
    tl.append(("dz0", "w_in", 6160, 512, 8))
    tl.append(("dz1", "w_in", 6672, 512, 8))
    tl.append(("ba", "w_in", 7184, 16, 8))
    tl.append(("gg0", "w_in", 7200, 512, 8))
    tl.append(("gg1", "w_in", 7712, 512, 8))
    tl.append(("gd0", "w_in", 8224, 512, 8))
    tl.append(("gd1", "w_in", 8736, 512, 8))
    for j in range(2):
        tl.append((f"bg{j}", "w_branch_gla", 512 * j, 512, 8))
    for j in range(2):
        tl.append((f"bd{j}", "w_branch_dn", 512 * j, 512, 8))
    for j in range(2):
        tl.append((f"wo{j}", "w_out", 512 * j, 512, 8))
    for j in range(8):
        tl.append((f"up{j}", "w_mlp_up", 512 * j, 512, 8))
    for hh in range(2):
        for j in range(4):
            tl.append((f"dn{hh}_{j}", "w_mlp_down", 256 * j, 256, 16, 2048 * hh))
    return [t if len(t) == 6 else t + (0,) for t in tl]


WT = w_tiles()
WIDX = {t[0]: i for i, t in enumerate(WT)}
NWT = len(WT)


import os as _os
KSTOP = float(_os.environ.get("KSTOP", "99"))
KV = int(_os.environ.get("KV", "0"))


def build_program(depth=DEPTH, nblocks=9, dbg=False):
    nc = bass.Bass("TRN2", target_bir_lowering=False)
    S = Sched()
    es = ExitStack()

    nreal = (nblocks - 1) * TB
    x_d = nc.dram_tensor("x", [max(nreal, 128), D], F32, kind="ExternalInput").ap()
    meta_d = nc.dram_tensor("meta", [NMETA, D], F32, kind="ExternalInput").ap()
    wd = {}
    wd["w_in"] = nc.dram_tensor("w_in", [depth, D, INC], F32, kind="ExternalInput").ap()
    wd["w_branch_gla"] = nc.dram_tensor("w_branch_gla", [depth, D, D], F32, kind="ExternalInput").ap()
    wd["w_branch_dn"] = nc.dram_tensor("w_branch_dn", [depth, D, D], F32, kind="ExternalInput").ap()
    wd["w_out"] = nc.dram_tensor("w_out", [depth, D, D], F32, kind="ExternalInput").ap()
    wd["w_mlp_up"] = nc.dram_tensor("w_mlp_up", [depth, D, DFF], F32, kind="ExternalInput").ap()
    wd["w_mlp_down"] = nc.dram_tensor("w_mlp_down", [depth, DFF, D], F32, kind="ExternalInput").ap()
    small_d = nc.dram_tensor("small", [depth, 128, NS], F32, kind="ExternalInput").ap()
    gup_d = nc.dram_tensor("gup", [depth, 16, 512], F32, kind="ExternalInput").ap()
    cm_d = nc.dram_tensor("cmask", [128, 20, 128], F32, kind="ExternalInput").ap()
    wfin_d = nc.dram_tensor("wfin", [128, D], F32, kind="ExternalInput").ap()
    out_d = nc.dram_tensor("out", [max(nreal, 128), D], F32, kind="ExternalOutput").ap()
    scr_d = nc.dram_tensor("wscr", [depth, NWT, 128, 4096], BF16, kind="Internal").ap()
    sst_d = nc.dram_tensor("sstate", [depth, 2, 128, 1024], F32, kind="Internal").ap()

    def sb(name, shape, dt):
        return es.enter_context(nc.sbuf_tensor("s_" + name, shape, dt))

    ps_t = es.enter_context(nc.psum_tensor("ps", [128, 4096], F32))
    ps_bf = ps_t.bitcast(BF16)
    PSB = [Buf(f"psb{i}", excl=True) for i in range(8)]

    def psf(b, n=512, off=0):
        return ps_t[:, b * 512 + off: b * 512 + off + n]

    def psh(b, n=1024, off=0):
        return ps_bf[:, b * 1024 + off: b * 1024 + off + n]

    class PSA:
        p1 = 0
        p2 = 0
        singles = list(range(8))
        pairs = [0, 2, 4, 6]

        @classmethod
        def mode(cls, m):
            if m == "dn":
                cls.singles, cls.pairs = [0, 1, 2, 3, 4, 5], [0, 2, 4]
            elif m == "gla":
                cls.singles, cls.pairs = [6, 7], [6]
            elif m == "h0":
                cls.singles, cls.pairs = [0, 1, 2, 3], [0, 2]
            elif m == "h1":
                cls.singles, cls.pairs = [4, 5, 6, 7], [4, 6]
            else:
                cls.singles, cls.pairs = list(range(8)), [0, 2, 4, 6]

        @classmethod
        def get1(cls):
            cls.p1 += 1
            return cls.singles[cls.p1 % len(cls.singles)]

        @classmethod
        def get2(cls):
            cls.p2 += 1
            return cls.pairs[cls.p2 % len(cls.pairs)]

    cm = sb("cm", [128, 5, 128], F32)
    cmb = sb("cmb", [128, 20, 128], BF16)
    B_cm = Buf("cm")
    wfin = sb("wfin", [128, D], F32)
    B_wfin = Buf("wfin")
    small = sb("small", [128, depth, NS], F32)
    B_small = Buf("small")
    negb = sb("negb", [128, depth, 4], F32)
    negA = sb("negA", [128, depth, 8], F32)
    gupb = sb("gupb", [16, 512], BF16)
    epsc = sb("epsc", [128, 1], F32)
    onec = sb("onec", [128, 1], F32)
    B_c1 = Buf("c1")

    ident_f = cm[:, 0, :]
    ident_b = cmb[:, 0, :]
    U_b = cmb[:, 1, :]
    SL_b = cmb[:, 2, :]
    ones_b = cmb[:, 4, :]
    ones_f = cm[:, 4, :]

    hT = sb("hT", [128, 8, TB], F32)
    B_h = Buf("hT")
    uT = sb("uT", [128, 8, TB], BF16)
    B_u = Buf("uT")
    Sg = sb("Sg", [128, 4, 256], F32)
    Sd = sb("Sd", [128, 8, 128], F32)
    B_Sg = Buf("Sg")
    B_Sd2 = [Buf("Sd0"), Buf("Sd1")]
    B_Sgd = [Buf(f"Sgd{l}") for l in range(depth)]
    B_Sdd = [Buf(f"Sdd{l}") for l in range(depth)]
    Sgb = sb("Sgb", [128, 4, 256], BF16)
    Sdb = sb("Sdb", [128, 8, 128], BF16)
    B_Sgb = Buf("Sgb")
    B_Sdb2 = [Buf("Sdb0"), Buf("Sdb1")]
    halo = sb("halo", [128, depth, 24, 4], BF16)
    B_halo = [Buf(f"halo{l}") for l in range(depth)]

    NSLOT = 3
    wring = [sb(f"wr{i}", [128, 4096], BF16) for i in range(NSLOT)]
    B_wr = [Buf(f"wr{i}") for i in range(NSLOT)]
    NGRP = 6
    B_scr = [[Buf(f"scr{l}_{g}") for g in range(NGRP)] for l in range(depth)]
    GRP = lambda ti: min(ti // 7, NGRP - 1)

    oTt = sb("oT", [128, 8, TB], BF16)
    oT = [oTt, oTt]
    B_oTt = Buf("oT")
    B_oT = [B_oTt, B_oTt]
    rstd = sb("rstd", [128, TB], F32)
    B_rstd = Buf("rstd")
    big = sb("big", [128, 24 * TB], BF16)
    B_big = Buf("big")
    B_sqn = B_big
    B_qk = [Buf(f"qk{g}") for g in range(8)]
    sqn = big[:, 0:8 * TB].rearrange("p (c t) -> p c t", c=8)
    vT_ = sb("vT", [128, 4, 1024], BF16)
    B_vT = Buf("vT")
    gz = sb("gz", [128, 4, 1024], BF16)
    B_gz = Buf("gz")
    ED = sb("ED", [128, 1024], BF16)
    EDT = sb("EDT", [128, 1024], BF16)
    B_ED, B_EDT = Buf("ED"), Buf("EDT")
    t1 = sb("t1", [128, 1024], F32)
    B_t1 = Buf("t1")
    mb = sb("mb", [128, 8, 128], BF16)
    B_mb = Buf("mb")
    usb = sb("usb", [128, 1024], F32)
    B_usb = Buf("usb")
    qe = sb("qe", [128, 4, TB], BF16)
    ke = sb("ke", [128, 4, TB], BF16)
    kdT = t1.bitcast(BF16)[:, 0:4 * TB].rearrange("p (c t) -> p c t", c=4)
    B_qe, B_ke, B_kdT = Buf("qe"), Buf("ke"), Buf("kdT")
    fa = ED.bitcast(F32)[:, 0:TB]
    fb = EDT.bitcast(F32)[:, 0:TB]
    B_fa, B_fb = Buf("fa"), Buf("fb")
    fc = sb("fc", [128, TB], F32)
    fd = sb("fd", [128, TB], F32)
    B_fc, B_fd = Buf("fc"), Buf("fd")
    def _h2(n):
        return [Buf(n + "0"), Buf(n + "1")]
    Bh = {n: _h2(n) for n in ("kbg", "kdd", "bv", "ED", "EDT", "t1", "mb", "Pm0", "Pm1", "Ptm0", "Ptm1", "Tt", "AqT",
                              "usb", "wTs", "vn", "ATm0", "ATm1")}

    xin = [usb, t1]
    yout = [usb, t1]
    B_xin_l = [Bh["usb"], Bh["t1"]]
    B_xin = [Bh["usb"][0], Bh["t1"][0]]
    B_yout = B_xin
    decg = sb("decg", [128, 4, 4], F32)
    B_decg = Buf("decg")
    glr = sb("glr", [16, TB], BF16)
    B_glr = Buf("glr")
    sc = sb("sc", [128, 512], BF16)
    B_sc = Buf("sc")
    kdt = sb("kdt", [128, 512], BF16)
    B_kdt = Buf("kdt")
    osq = sb("osq", [128, 1024], BF16)
    B_osq = Buf("osq")
    oss = sb("oss", [128, 8], F32)
    orr = sb("orr", [128, 8], F32)
    B_oss, B_orr = Buf("oss"), Buf("orr")
    onf = sb("onf", [128, 1024], F32)
    B_onf = Buf("onf")
    onz = sb("onz", [128, 1024], BF16)
    B_onz = Buf("onz")
    xraw = [sb(f"xraw{i}", [128, TB + 4], BF16) for i in range(2)]
    B_xraw = [Buf(f"xraw{i}") for i in range(2)]
    dg = [sb(f"dg{i}", [128, 4, 128], BF16) for i in range(2)]
    B_dg = [Buf(f"dg{i}") for i in range(2)]
    ba = sb("ba", [128, 4, 16], F32)
    B_ba = Buf("ba")
    sca4 = sb("sca4", [128, 9, 4, 8], F32)
    B_sca = Buf("sca")
    lab4 = sb("lab4", [128, 4, 8], BF16)
    B_lab = Buf("lab")
    RU = sb("RU", [128, 8, 128], BF16)
    RSL = sb("RSL", [128, 8, 128], BF16)
    B_RU, B_RSL = Buf("RU"), Buf("RSL")
    kbg = sb("kbg", [128, 8, 128], BF16)
    kdd = sb("kdd", [128, 8, 128], BF16)
    bv = sb("bv", [128, 8, 128], BF16)
    B_kbg, B_kdd, B_bv = Buf("kbg"), Buf("kdd"), Buf("bv")
    Pm = [sb(f"Pm{i}", [128, 8, 128], BF16) for i in range(2)]
    Ptm = [sb(f"Ptm{i}", [128, 8, 128], BF16) for i in range(2)]
    B_Pm = [Buf(f"Pm{i}") for i in range(2)]
    B_Ptm = [Buf(f"Ptm{i}") for i in range(2)]
    Tt = sb("Tt", [128, 8, 128], BF16)
    B_Tt = Buf("Tt")
    ATm = [sb(f"ATm{i}", [128, 8, 128], BF16) for i in range(2)]
    AqT = sb("AqT", [128, 8, 128], BF16)
    B_AqT = Buf("AqT")
    wTs = sb("wTs", [128, 8, 128], BF16)
    B_wTs = Buf("wTs")
    vn = sb("vn", [128, 8, 128], BF16)
    B_vn = Buf("vn")

    qkvT = big[:, 0:24 * TB].rearrange("p (c t) -> p c t", c=24)
    h2T = big[:, 0:16 * TB].rearrange("p (c t) -> p c t", c=16)
    zs = vT_
    B_zs = B_vT
    mrg = gz[:, :, :].rearrange("p a b -> p (a b)").rearrange("p (c t) -> p c t", c=8)
    B_mrg = B_gz
    sg2 = zs[:, :, :].rearrange("p a b -> p (a b)").rearrange("p (c t) -> p c t", c=8)
    B_sg2 = B_zs

    sm = lambda l, a, b: small[:, l, a:b]
    taps = {}

    def tap(name, ap, bufs, cond=True):
        if not dbg or not cond or name in taps:
            return
        shape = [int(x) for x in ap.shape]
        dt = ap.dtype
        d = nc.dram_tensor("dbg_" + name, shape, dt, kind="ExternalOutput").ap()
        b = Buf("dbg_" + name)
        taps[name] = b
        S.dma("sp", lambda h: h.dma_start(out=d, in_=ap), b, r=list(bufs), w=[b])

    S.dma("sp", lambda h: h.dma_start(out=cm[:], in_=cm_d[:, 0:5, :]), B_cm, w=[B_cm])
    S.dma("sp", lambda h: h.dma_start(out=wfin[:], in_=wfin_d), B_wfin, w=[B_wfin])
    S.dma("sp", lambda h: h.dma_start(out=small[:], in_=small_d.rearrange("l p n -> p l n")), B_small, w=[B_small])
    B_gupb = Buf("gupb")
    B_cmb = Buf("cmb")
    S.dma("pool", lambda h: h.dma_start(out=cmb[:], in_=cm_d), B_cmb, w=[B_cmb])
    S.add("dve", lambda h: h.memset(epsc[:], EPS), w=[B_c1])
    S.add("dve", lambda h: h.memset(onec[:], 1.0), w=[B_c1])
    B_neg = Buf("neg")
    S.add("dve", lambda h: h.tensor_scalar(out=negb[:], in0=small[:, :, 16:20], scalar1=-1.0, scalar2=None, op0=ALU.mult),
          r=[B_small], w=[B_neg])
    S.add("act", lambda h: h.activation(out=negA[:], in_=small[:, :, 132:140], func=AF.Exp), r=[B_small], w=[B_neg])
    S.add("dve", lambda h: h.tensor_scalar(out=negA[:], in0=negA[:], scalar1=-1.0, scalar2=None, op0=ALU.mult),
          r=[B_neg], w=[B_neg])
    S.add("dve", lambda h: h.memset(halo[:], 0.0), w=B_halo)

    def convert_tile(l, ti):
        nm, src, c0, ncol, kc, r0 = WT[ti]

        def f(h):
            s_ap = wd[src][l, r0:r0 + 128 * kc, c0:c0 + ncol].rearrange("(kc p) c -> p kc c", p=128)
            d_ap = scr_d[l, ti, :, 0:kc * ncol].rearrange("p (kc c) -> p kc c", kc=kc)
            return h.dma_start(out=d_ap, in_=s_ap)
        S.dma("pool", f, B_scr[l][GRP(ti)], w=[B_scr[l][GRP(ti)]], nowaw=True)

    for ti in range(NWT):
        convert_tile(0, ti)
    CUR = {"blk": 0}

    class WR:
        nxt = 0

    def wload(l, nm):
        ti = WIDX[nm]
        _, src, c0, ncol, kc, _r0 = WT[ti]
        s = WR.nxt
        WR.nxt = (WR.nxt + 1) % NSLOT
        n = kc * ncol

        def f(h):
            return h.dma_start(out=wring[s][:, 0:n], in_=scr_d[l, ti, :, 0:n])
        S.dma("sp", f, B_wr[s], r=[B_scr[l][GRP(ti)]], w=[B_wr[s]])
        if CUR["blk"] == 0 and l + 1 < depth:
            convert_tile(l + 1, ti)
        return B_wr[s], wring[s][:, 0:n].rearrange("p (kc c) -> p kc c", kc=kc)

    def mm_group(items, r, w):
        def f(h):
            ins = None
            for (o, a, b, st, sp) in items:
                ins = h.matmul(o, a, b, start=st, stop=sp)
            return ins
        S.add("pe", f, r=r, w=w)

    def tr_group(items, r, w):
        def f(h):
            ins = None
            for (o, a, idn) in items:
                ins = h.transpose(o, a, idn)
            return ins
        S.add("pe", f, r=r, w=w)

    def proj_F(wb, wt, c_lo, m, inT, B_in, kcn, ntok, bank):
        items = []
        for k in range(kcn):
            items.append((psf(bank)[0:m, 0:ntok], wt[:, k, c_lo:c_lo + m], inT[:, k, 0:ntok], k == 0, k == kcn - 1))
        mm_group(items, r=[wb, B_in], w=[PSB[bank]])

    def proj_T(wb, wt, ncol, i, bank):
        items = []
        for k in range(KC):
            items.append((psf(bank)[:, 0:ncol], uT[:, k, 128 * i:128 * i + 128], wt[:, k, 0:ncol], k == 0, k == KC - 1))
        mm_group(items, r=[wb, B_u], w=[PSB[bank]])

    def norm_F(wcols, ntok):
        S.add("act", lambda h: h.activation(out=sqn[:, :, 0:ntok], in_=hT[:, :, 0:ntok], func=AF.Square),
              r=[B_h], w=[B_sqn])
        bank = PSA.get1()
        mm_group([(psf(bank)[:, 0:ntok], ones_b, sqn[:, k, 0:ntok], k == 0, k == 7) for k in range(8)],
                 r=[B_sqn, B_cmb], w=[PSB[bank]])
        S.add("act", lambda h: h.activation(out=rstd[:, 0:ntok], in_=psf(bank)[:, 0:ntok], func=AF.Ln,
                                            bias=epsc[:, 0:1], scale=1.0 / D),
              r=[PSB[bank], B_c1], w=[B_rstd])
        S.add("act", lambda h: h.activation(out=rstd[:, 0:ntok], in_=rstd[:, 0:ntok], func=AF.Exp, scale=-0.5),
              r=[B_rstd], w=[B_rstd])
        for k in range(8):
            S.add("dve", lambda h, k=k: h.scalar_tensor_tensor(
                out=uT[:, k, 0:ntok], in0=hT[:, k, 0:ntok], scalar=wcols[:, k:k + 1],
                in1=rstd[:, 0:ntok], op0=ALU.mult, op1=ALU.mult),
                r=[B_h, B_rstd, B_small], w=[B_u])

    B_osq_h = [Buf("osq0"), Buf("osq1")]
    B_oss_h = [Buf("oss0"), Buf("oss1")]
    B_orr_h = [Buf("orr0"), Buf("orr1")]
    B_onf_h = [Buf("onf0"), Buf("onf1")]
    B_onz_h = [Buf("onz0"), Buf("onz1")]

    def out_path(o_ap, B_o, H, dv, gate_ap, B_gate, nwcols, which, i, half=None):
        if half is None:
            W, off, nch, cb, hb = 1024, 0, 8, 0, 0
            bs = lambda L: list(L)
        else:
            W, off, nch, cb, hb = 512, 512 * half, 4, 4 * half, 4 * half
            bs = lambda L: [L[half]]
        sq_ = osq[:, off:off + W]
        on_ = onf[:, off:off + W]
        oz_ = onz[:, off:off + W]
        ss_ = oss[:, hb:hb + H]
        rr_ = orr[:, hb:hb + H]
        S.add("act", lambda h: h.activation(out=sq_, in_=o_ap, func=AF.Square), r=B_o, w=bs(B_osq_h))
        S.add("dve", lambda h: h.tensor_reduce(out=ss_, in_=sq_.rearrange("p (h v) -> p h v", h=H),
                                               axis=AX.X, op=ALU.add), r=bs(B_osq_h), w=bs(B_oss_h))
        S.add("act", lambda h: h.activation(out=rr_, in_=ss_, func=AF.Ln, bias=epsc[:, 0:1],
                                            scale=1.0 / dv), r=bs(B_oss_h) + [B_c1], w=bs(B_orr_h))
        S.add("act", lambda h: h.activation(out=rr_, in_=rr_, func=AF.Exp, scale=-0.5),
              r=bs(B_orr_h), w=bs(B_orr_h))
        S.add("dve", lambda h: h.tensor_tensor(out=on_.rearrange("p (h v) -> p h v", h=H),
                                               in0=o_ap.rearrange("p (h v) -> p h v", h=H),
                                               in1=bc_last(rr_, dv), op=ALU.mult),
              r=list(B_o) + bs(B_orr_h), w=bs(B_onf_h))
        S.add("pool", lambda h: h.tensor_tensor(out=oz_, in0=on_, in1=gate_ap, op=ALU.mult),
              r=bs(B_onf_h) + [B_gate], w=bs(B_onz_h))
        bank = PSA.get1()
        tr_group([(psh(bank)[:, j * 128:(j + 1) * 128], oz_[:, j * 128:(j + 1) * 128], ident_b) for j in range(nch)],
                 r=bs(B_onz_h) + [B_cmb], w=[PSB[bank]])
        S.add("dve", lambda h: h.tensor_tensor(out=oT[which][:, cb:cb + nch, 128 * i:128 * i + 128],
                                               in0=psh(bank)[:, 0:nch * 128].rearrange("p (c t) -> p c t", c=nch),
                                               in1=bc_last(nwcols, 128), op=ALU.mult),
              r=[PSB[bank], B_small], w=[B_oT[which]])

    def gla_front(l, ntok, nt):
        dk = 128
        S.dma("pool", lambda h: h.dma_start(out=gupb[:], in_=gup_d[l]), B_gupb, w=[B_gupb])
        wb, wt = wload(l, "lr")
        tap("wlr", wt, [wb], nt == 4)
        bank = PSA.get1()
        proj_F(wb, wt, 0, 16, uT, B_u, KC, ntok, bank)
        S.add("act", lambda h, bank=bank: h.activation(out=glr[:, 0:ntok], in_=psf(bank)[0:16, 0:ntok], func=AF.Copy),
              r=[PSB[bank]], w=[B_glr])
        tap("glr", glr[:, 0:ntok], [B_glr], nt == 4)
        wbq, wq = wload(l, "q")
        wbk, wk = wload(l, "k")
        for hd in range(4):
            bz = PSA.get1()
            mm_group([(psf(bz)[:, 0:ntok], gupb[0:16, hd * 128:(hd + 1) * 128], glr[0:16, 0:ntok], True, True)],
                     r=[B_gupb, B_glr], w=[PSB[bz]])
            S.add("act", lambda h, hd=hd, bz=bz: h.activation(out=fa[:, 0:ntok], in_=psf(bz)[:, 0:ntok], func=AF.Exp,
                                                              bias=negb[:, l, hd:hd + 1], scale=-1.0),
                  r=[PSB[bz], B_neg], w=[B_fa])
            S.add("act", lambda h: h.activation(out=fa[:, 0:ntok], in_=fa[:, 0:ntok], func=AF.Ln,
                                                bias=onec[:, 0:1], scale=1.0), r=[B_fa, B_c1], w=[B_fa])
            S.add("dve", lambda h: h.tensor_scalar(out=fb[:, 0:ntok], in0=fa[:, 0:ntok], scalar1=-1.0 / 16.0,
                                                   scalar2=-1.0, op0=ALU.mult, op1=ALU.max), r=[B_fa], w=[B_fb])
            tap("la", fb[:, 0:ntok], [B_fb], hd == 3 and nt == 4)
            for i in range(nt):
                S.add("dve", lambda h, i=i: h.tensor_tensor_scan(
                    out=fc[:, 128 * i:128 * i + 128], data0=ones_f, data1=fb[:, 128 * i:128 * i + 128],
                    initial=0.0, op0=ALU.mult, op1=ALU.add), r=[B_fb, B_cm], w=[B_fc])
            S.add("act", lambda h: h.activation(out=fa[:, 0:ntok], in_=fc[:, 0:ntok], func=AF.Exp), r=[B_fc], w=[B_fa])
            S.add("act", lambda h: h.activation(out=fd[:, 0:ntok], in_=fc[:, 0:ntok], func=AF.Exp, scale=-1.0),
                  r=[B_fc], w=[B_fd])
            S.add("dve", lambda h, hd=hd: h.tensor_copy(
                out=decg[:, hd, 0:nt], in_=fa[:, 0:ntok].rearrange("p (i t) -> p i t", t=128)[:, :, 127]),
                r=[B_fa], w=[B_decg])
            tap("cum", fc[:, 0:ntok], [B_fc], hd == 3 and nt == 4)
            tap("E1", fa[:, 0:ntok], [B_fa], hd == 3 and nt == 4)
            tap("decg", decg[:], [B_decg], hd == 3 and nt == 4)
            bq = PSA.get1()
            proj_F(wbq, wq, hd * 128, 128, uT, B_u, KC, ntok, bq)
            S.add("dve", lambda h, hd=hd, bq=bq: h.scalar_tensor_tensor(
                out=qe[:, hd, 0:ntok], in0=psf(bq)[:, 0:ntok], scalar=float(dk) ** -0.5, in1=fa[:, 0:ntok],
                op0=ALU.mult, op1=ALU.mult), r=[PSB[bq], B_fa], w=[B_qe])
            bk = PSA.get1()
            proj_F(wbk, wk, hd * 128, 128, uT, B_u, KC, ntok, bk)
            S.add("dve", lambda h, hd=hd, bk=bk: h.tensor_tensor(
                out=ke[:, hd, 0:ntok], in0=psf(bk)[:, 0:ntok], in1=fd[:, 0:ntok], op=ALU.mult),
                r=[PSB[bk], B_fd], w=[B_ke])
            for i in range(nt):
                S.add("pool", lambda h, hd=hd, i=i: h.tensor_scalar(
                    out=kdT[:, hd, 128 * i:128 * i + 128], in0=ke[:, hd, 128 * i:128 * i + 128],
                    scalar1=decg[:, hd, i:i + 1], scalar2=1.0, op0=ALU.mult, op1=ALU.mult),
                    r=[B_ke, B_decg], w=[B_kdT])
        tap("qe", qe[:, :, 0:ntok], [B_qe], nt == 4)
        tap("ke", ke[:, :, 0:ntok], [B_ke], nt == 4)
        tap("kdT", kdT[:, :, 0:ntok], [B_kdT], nt == 4)
        tap("uT", uT[:], [B_u], nt == 4)
        for g in range(2):
            wb, wt = wload(l, f"v{g}")
            for i in range(nt):
                bank = PSA.get1()
                proj_T(wb, wt, 512, i, bank)
                S.add("act", lambda h, i=i, g=g, bank=bank: h.activation(
                    out=vT_[:, i, g * 512:(g + 1) * 512], in_=psf(bank), func=AF.Copy), r=[PSB[bank]], w=[B_vT])
        for g in range(2):
            wb, wt = wload(l, f"gz{g}")
            for i in range(nt):
                bank = PSA.get1()
                proj_T(wb, wt, 512, i, bank)
                S.add("act", lambda h, i=i, g=g, bank=bank: h.activation(
                    out=gz[:, i, g * 512:(g + 1) * 512], in_=psf(bank), func=AF.Silu), r=[PSB[bank]], w=[B_gz])
    def gla_loop(l, ntok, nt, first):
        if first:
            S.add("dve", lambda h: h.memset(Sg[:], 0.0), w=[B_Sg])
        else:
            S.dma("sp", lambda h: h.dma_start(out=Sg[:].rearrange("p a b -> p (a b)"), in_=sst_d[l, 0]), B_Sg,
                  r=[B_Sgd[l]], w=[B_Sg])
        S.add("act", lambda h: h.activation(out=Sgb[:], in_=Sg[:], func=AF.Copy), r=[B_Sg], w=[B_Sgb])
        for i in range(nt):
            tsl = slice(128 * i, 128 * i + 128)
            b1 = PSA.get1()
            mm_group([(psf(b1)[:, hd * 128:(hd + 1) * 128], ke[:, hd, tsl], qe[:, hd, tsl], True, True) for hd in range(4)],
                     r=[B_ke, B_qe], w=[PSB[b1]])
            S.add("dve", lambda h, b1=b1: h.tensor_tensor(
                out=sc[:].rearrange("p (a b) -> p a b", a=4), in0=psf(b1).rearrange("p (a b) -> p a b", a=4),
                in1=bc_mid(cm[:, 1, :], 4), op=ALU.mult), r=[PSB[b1], B_cm], w=[B_sc])
            b2 = PSA.get1()
            tr_group([(psh(b2)[:, hd * 128:(hd + 1) * 128], kdT[:, hd, tsl], ident_b) for hd in range(4)],
                     r=[B_kdT, B_cmb], w=[PSB[b2]])
            S.add("act", lambda h, b2=b2: h.activation(out=kdt[:], in_=psh(b2)[:, 0:512], func=AF.Copy),
                  r=[PSB[b2]], w=[B_kdt])
            b3 = PSA.get2()
            items = []
            for hd in range(4):
                o = ps_t[:, b3 * 512 + hd * 256: b3 * 512 + (hd + 1) * 256]
                items.append((o, sc[:, hd * 128:(hd + 1) * 128], vT_[:, i, hd * 256:(hd + 1) * 256], True, False))
                items.append((o, qe[:, hd, tsl], Sgb[:, hd, :], False, True))
            mm_group(items, r=[B_sc, B_vT, B_qe, B_Sgb], w=[PSB[b3], PSB[b3 + 1]])
            b4 = PSA.get2()
            mm_group([(ps_t[:, b4 * 512 + hd * 256: b4 * 512 + (hd + 1) * 256], kdt[:, hd * 128:(hd + 1) * 128],
                       vT_[:, i, hd * 256:(hd + 1) * 256], True, True) for hd in range(4)],
                     r=[B_kdt, B_vT], w=[PSB[b4], PSB[b4 + 1]])
            for hd in range(4):
                S.add("dve", lambda h, hd=hd, b4=b4, i=i: h.scalar_tensor_tensor(
                    out=Sg[:, hd, :], in0=Sg[:, hd, :], scalar=decg[:, hd, i:i + 1],
                    in1=ps_t[:, b4 * 512 + hd * 256: b4 * 512 + (hd + 1) * 256], op0=ALU.mult, op1=ALU.add),
                    r=[B_Sg, B_decg, PSB[b4], PSB[b4 + 1]], w=[B_Sg])
            if i < nt - 1:
                S.add("act", lambda h: h.activation(out=Sgb[:], in_=Sg[:], func=AF.Copy), r=[B_Sg], w=[B_Sgb])
            out_path(ps_t[:, b3 * 512:b3 * 512 + 1024], [PSB[b3], PSB[b3 + 1]], 4, 256,
                     gz[:, i, :], B_gz, sm(l, 20, 28), 0, i)

        S.dma("sp", lambda h: h.dma_start(out=sst_d[l, 0], in_=Sg[:].rearrange("p a b -> p (a b)")), B_Sg,
              r=[B_Sg], w=[B_Sgd[l]])

    def dn_conv(l, ntok, nt):
        st = {"wb": None, "wt": None}
        b1 = {}

        def stage1(c):
            if c % 4 == 0:
                st["wb"], st["wt"] = wload(l, f"dqkv{c // 4}")
            bank = PSA.get1()
            proj_F(st["wb"], st["wt"], (c % 4) * 128, 128, uT, B_u, KC, ntok, bank)
            xs = c % 2
            S.add("act", lambda h: h.activation(out=xraw[xs][:, 3:3 + ntok], in_=psf(bank)[:, 0:ntok],
                                                func=AF.Copy), r=[PSB[bank]], w=[B_xraw[xs]])
            S.add("pool", lambda h: h.tensor_copy(out=xraw[xs][:, 0:3], in_=halo[:, l, c, 0:3]),
                  r=[B_halo[l]], w=[B_xraw[xs]])
            S.add("pool", lambda h: h.tensor_copy(out=halo[:, l, c, 0:3], in_=xraw[xs][:, ntok:ntok + 3]),
                  r=[B_xraw[xs]], w=[B_halo[l]])
            S.add("dve", lambda h: h.tensor_tensor(
                out=dg[xs][:], in0=bc_mid(cm[:, 0, :], 4), in1=bc_last(small[:, l, 36 + 4 * c:40 + 4 * c], 128),
                op=ALU.mult), r=[B_cm, B_small], w=[B_dg[xs]])

        def stage2(c):
            xs = c % 2
            b2 = PSA.get1()
            mm_group([(psf(b2)[:, 0:ntok], dg[xs][:, j, :], xraw[xs][:, j:j + ntok], j == 0, j == 3) for j in range(4)],
                     r=[B_dg[xs], B_xraw[xs]], w=[PSB[b2]])
            S.add("act", lambda h: h.activation(out=qkvT[:, c, 0:ntok], in_=psf(b2)[:, 0:ntok], func=AF.Silu),
                  r=[PSB[b2]], w=[B_big])

        stage1(0)
        for c in range(24):
            if c + 1 < 24:
                stage1(c + 1)
            stage2(c)

    def dn_front(l, ntok, nt):
        dk = 128
        if KSTOP < 2.2:
            return
        for g in range(2):
            wb, wt = wload(l, f"dz{g}")
            for i in range(nt):
                bank = PSA.get1()
                proj_T(wb, wt, 512, i, bank)
                S.add("act", lambda h, i=i, g=g, bank=bank: h.activation(
                    out=zs[:, i, g * 512:(g + 1) * 512], in_=psf(bank), func=AF.Silu), r=[PSB[bank]], w=[B_zs])
        wb, wt = wload(l, "ba")
        for i in range(nt):
            bank = PSA.get1()
            proj_T(wb, wt, 16, i, bank)
            S.add("act", lambda h, i=i, bank=bank: h.activation(out=ba[:, i, :], in_=psf(bank)[:, 0:16], func=AF.Copy),
                  r=[PSB[bank]], w=[B_ba])
        if KSTOP < 2.3:
            return
        sqs = [osq[:, :].rearrange("p (a t) -> p a t", a=2), onz[:, :].rearrange("p (a t) -> p a t", a=2)]
        B_sqs = [B_osq_h, B_onz_h]
        rss = [usb[:, :].rearrange("p (a t) -> p a t", a=2), t1[:, :].rearrange("p (a t) -> p a t", a=2)]
        B_rss = [Bh["usb"], Bh["t1"]]
        for gi, c in enumerate(range(0, 16, 2)):
            B_g = B_qk[gi]
            sq_, B_sq = sqs[gi % 2], B_sqs[gi % 2]
            rs_, B_rs = rss[gi % 2], B_rss[gi % 2]
            S.add("pool", lambda h, c=c, sq_=sq_: h.tensor_tensor(out=sq_[:, :, 0:ntok], in0=qkvT[:, c:c + 2, 0:ntok],
                                                                  in1=qkvT[:, c:c + 2, 0:ntok], op=ALU.mult),
                  r=[B_big, B_g], w=B_sq)
            bp = PSA.get2()
            mm_group([(ps_t[:, (bp + a) * 512:(bp + a) * 512 + ntok], ones_b, sq_[:, a, 0:ntok], True, True) for a in range(2)],
                     r=B_sq + [B_cmb], w=[PSB[bp], PSB[bp + 1]])
            pv = ps_t[:, bp * 512:bp * 512 + 1024].rearrange("p (a t) -> p a t", a=2)[:, :, 0:ntok]
            S.add("act", lambda h, pv=pv, rs_=rs_: h.activation(out=rs_[:, :, 0:ntok], in_=pv, func=AF.Ln,
                                                                bias=epsc[:, 0:1], scale=1.0),
                  r=[PSB[bp], PSB[bp + 1], B_c1], w=B_rs)
            S.add("act", lambda h, rs_=rs_: h.activation(out=rs_[:, :, 0:ntok], in_=rs_[:, :, 0:ntok], func=AF.Exp, scale=-0.5),
                  r=B_rs, w=B_rs)
            S.add("dve", lambda h, c=c, rs_=rs_: h.tensor_tensor(out=qkvT[:, c:c + 2, 0:ntok], in0=qkvT[:, c:c + 2, 0:ntok],
                                                                 in1=rs_[:, :, 0:ntok], op=ALU.mult), r=[B_g] + B_rs, w=[B_g])
    def dn_half(l, i, hf, tsl, R):
        h0 = 4 * hf
        HS = slice(h0, h0 + 4)
        FS = slice(512 * hf, 512 * hf + 512)
        fl = lambda t: t[:, HS, :].rearrange("p a b -> p (a b)")
        p3 = lambda b: psh(b)[:, 0:512].rearrange("p (a b) -> p a b", a=4)
        f3 = lambda b: psf(b).rearrange("p (a b) -> p a b", a=4)
        B = {k: v[hf] for k, v in Bh.items()}
        B_Sdh, B_Sdbh = B_Sd2[hf], B_Sdb2[hf]
        bk = PSA.get1()
        tr_group([(psh(bk)[:, j * 128:(j + 1) * 128], qkvT[:, 8 + h0 + j, tsl], ident_b) for j in range(4)],
                 r=[B_big, B_cmb] + B_qk, w=[PSB[bk]])
        bvb = PSA.get1()
        tr_group([(psh(bvb)[:, j * 128:(j + 1) * 128], qkvT[:, 16 + h0 + j, tsl], ident_b) for j in range(4)],
                 r=[B_big, B_cmb], w=[PSB[bvb]])
        S.add("dve", lambda h: h.tensor_tensor(out=kbg[:, HS, :], in0=p3(bk), in1=bc_last(R(5)[:, HS], 128), op=ALU.mult),
              r=[PSB[bk], B_sca], w=[B["kbg"]])
        S.add("dve", lambda h: h.tensor_tensor(out=kdd[:, HS, :], in0=p3(bk), in1=bc_last(R(3)[:, HS], 128), op=ALU.mult),
              r=[PSB[bk], B_sca], w=[B["kdd"]])
        S.add("dve", lambda h: h.tensor_tensor(out=bv[:, HS, :], in0=p3(bvb), in1=bc_last(R(0)[:, HS], 128), op=ALU.mult),
              r=[PSB[bvb], B_sca], w=[B["bv"]])
        bD = PSA.get1()
        mm_group([(psf(bD), U_b, fl(RSL), True, True)], r=[B_cmb, B_RSL], w=[PSB[bD]])
        S.add("act", lambda h: h.activation(out=ED[:, FS], in_=psf(bD), func=AF.Exp), r=[PSB[bD]], w=[B["ED"]])
        bDT = PSA.get1()
        mm_group([(psf(bDT), SL_b, fl(RU), True, True)], r=[B_cmb, B_RU], w=[PSB[bDT]])
        S.add("act", lambda h: h.activation(out=EDT[:, FS], in_=psf(bDT), func=AF.Exp), r=[PSB[bDT]], w=[B["EDT"]])
        bG = PSA.get1()
        mm_group([(psf(bG)[:, j * 128:(j + 1) * 128], qkvT[:, 8 + h0 + j, tsl], qkvT[:, 8 + h0 + j, tsl], True, True)
                  for j in range(4)], r=[B_big] + B_qk, w=[PSB[bG]])
        S.add("dve", lambda h: h.tensor_tensor(out=t1[:, FS], in0=psf(bG), in1=ED[:, FS], op=ALU.mult),
              r=[PSB[bG], B["ED"]], w=[B["t1"]])
        S.add("pool", lambda h: h.tensor_tensor(out=mb[:, HS, :], in0=bc_mid(cmb[:, 2, :], 4), in1=bc_last(R(0)[:, HS], 128),
                                                op=ALU.mult), r=[B_cmb, B_sca], w=[B["mb"]])
        S.add("pool", lambda h: h.tensor_tensor(out=fl(Pm[0]), in0=t1[:, FS], in1=fl(mb), op=ALU.mult),
              r=[B["t1"], B["mb"]], w=[B["Pm0"]])
        bQ = PSA.get1()
        mm_group([(psf(bQ)[:, j * 128:(j + 1) * 128], qkvT[:, 8 + h0 + j, tsl], qkvT[:, h0 + j, tsl], True, True)
                  for j in range(4)], r=[B_big] + B_qk, w=[PSB[bQ]])
        S.add("dve", lambda h: h.tensor_tensor(out=t1[:, FS], in0=psf(bQ), in1=EDT[:, FS], op=ALU.mult),
              r=[PSB[bQ], B["EDT"]], w=[B["t1"]])
        S.add("pool", lambda h: h.tensor_tensor(out=AqT[:, HS, :], in0=t1[:, FS].rearrange("p (a b) -> p a b", a=4),
                                                in1=bc_mid(cm[:, 3, :], 4), op=ALU.mult), r=[B["t1"], B_cm], w=[B["AqT"]])
        bT = PSA.get1()
        tr_group([(psh(bT)[:, j * 128:(j + 1) * 128], Pm[0][:, h0 + j, :], ident_b) for j in range(4)],
                 r=[B["Pm0"], B_cmb], w=[PSB[bT]])
        S.add("act", lambda h: h.activation(out=fl(Ptm[0]), in_=psh(bT)[:, 0:512], func=AF.Copy),
              r=[PSB[bT]], w=[B["Ptm0"]])
        Tm = Pm[1]
        M1 = Ptm[1]
        S.add("pool", lambda h: h.tensor_tensor(out=M1[:, HS, :], in0=Pm[0][:, HS, :], in1=bc_mid(cmb[:, 6, :], 4), op=ALU.mult),
              r=[B["Pm0"], B_cmb], w=[B["Ptm1"]])
        S.add("dve", lambda h: h.tensor_tensor(out=Tm[:, HS, :], in0=bc_mid(cmb[:, 0, :], 4), in1=M1[:, HS, :], op=ALU.subtract),
              r=[B["Ptm1"], B_cmb], w=[B["Pm1"]])
        S.add("pool", lambda h: h.tensor_tensor(out=vn[:, HS, :], in0=Ptm[0][:, HS, :], in1=bc_mid(cmb[:, 5, :], 4), op=ALU.mult),
              r=[B["Ptm0"], B_cmb], w=[B["vn"]])
        S.add("dve", lambda h: h.tensor_tensor(out=Tt[:, HS, :], in0=bc_mid(cmb[:, 0, :], 4), in1=vn[:, HS, :], op=ALU.subtract),
              r=[B["vn"], B_cmb], w=[B["Tt"]])
        for j in range(1, 7):
            ATl = ATm[j % 2]
            S.add("pool", lambda h, j=j, ATl=ATl: h.tensor_tensor(out=ATl[:, HS, :], in0=Ptm[0][:, HS, :],
                                                                  in1=bc_mid(cmb[:, 13 + j, :], 4), op=ALU.mult),
                  r=[B["Ptm0"], B_cmb], w=[Bh["ATm%d" % (j % 2)][hf]])
            bM = PSA.get1()
            mm_group([(psf(bM)[:, q * 128:(q + 1) * 128], ATl[:, h0 + q, :], Tm[:, h0 + q, :], True, True) for q in range(4)],
                     r=[Bh["ATm%d" % (j % 2)][hf], B["Pm1"]], w=[PSB[bM]])
            S.add("act", lambda h, bM=bM: h.activation(out=fl(M1), in_=psf(bM), func=AF.Copy),
                  r=[PSB[bM]], w=[B["Ptm1"]])
            if j < 6:
                bdT = PSA.get1()
                mm_group([(psf(bdT)[:, q * 128:(q + 1) * 128], Tt[:, h0 + q, :], M1[:, h0 + q, :], True, True) for q in range(4)],
                         r=[B["Tt"], B["Ptm1"]], w=[PSB[bdT]])
            bdTt = PSA.get1()
            mm_group([(psf(bdTt)[:, q * 128:(q + 1) * 128], M1[:, h0 + q, :], Tt[:, h0 + q, :], True, True) for q in range(4)],
                     r=[B["Tt"], B["Ptm1"]], w=[PSB[bdTt]])
            if j < 6:
                S.add("dve", lambda h, bdT=bdT: h.tensor_tensor(out=fl(Tm), in0=fl(Tm), in1=psf(bdT), op=ALU.subtract),
                      r=[PSB[bdT], B["Pm1"]], w=[B["Pm1"]])
            S.add("dve", lambda h, bdTt=bdTt: h.tensor_tensor(out=fl(Tt), in0=fl(Tt), in1=psf(bdTt), op=ALU.subtract),
                  r=[PSB[bdTt], B["Tt"]], w=[B["Tt"]])
        bU = PSA.get1()
        mm_group([(psf(bU)[:, q * 128:(q + 1) * 128], Tt[:, h0 + q, :], bv[:, h0 + q, :], True, True) for q in range(4)],
                 r=[B["Tt"], B["bv"]], w=[PSB[bU]])
        S.add("act", lambda h: h.activation(out=usb[:, FS], in_=psf(bU), func=AF.Copy), r=[PSB[bU]], w=[B["usb"]])
        bW = PSA.get1()
        mm_group([(psf(bW)[:, q * 128:(q + 1) * 128], kbg[:, h0 + q, :], Tt[:, h0 + q, :], True, True) for q in range(4)],
                 r=[B["Tt"], B["kbg"]], w=[PSB[bW]])
        S.add("act", lambda h: h.activation(out=fl(wTs), in_=psf(bW), func=AF.Copy), r=[PSB[bW]], w=[B["wTs"]])
        bS = PSA.get1()
        mm_group([(psf(bS)[:, q * 128:(q + 1) * 128], wTs[:, h0 + q, :], Sdb[:, h0 + q, :], True, True) for q in range(4)],
                 r=[B["wTs"], B_Sdbh], w=[PSB[bS]])
        S.add("dve", lambda h: h.tensor_tensor(out=fl(vn), in0=usb[:, FS], in1=psf(bS), op=ALU.subtract),
              r=[PSB[bS], B["usb"]], w=[B["vn"]])
        bA = PSA.get1()
        mm_group([(psf(bA)[:, q * 128:(q + 1) * 128], qkvT[:, h0 + q, tsl], Sdb[:, h0 + q, :], True, True) for q in range(4)],
                 r=[B_big, B_Sdbh] + B_qk, w=[PSB[bA]])
        S.add("dve", lambda h: h.tensor_tensor(out=t1[:, FS].rearrange("p (a b) -> p a b", a=4), in0=f3(bA),
                                               in1=bc_last(R(8)[:, HS], 128), op=ALU.mult),
              r=[PSB[bA], B_sca], w=[B["t1"]])
        bB = PSA.get1()
        mm_group([(psf(bB)[:, q * 128:(q + 1) * 128], AqT[:, h0 + q, :], vn[:, h0 + q, :], True, True) for q in range(4)],
                 r=[B["AqT"], B["vn"]], w=[PSB[bB]])
        S.add("dve", lambda h: h.tensor_tensor(out=usb[:, FS], in0=psf(bB), in1=t1[:, FS], op=ALU.add),
              r=[PSB[bB], B["t1"]], w=[B["usb"]])
        bdS = PSA.get1()
        mm_group([(psf(bdS)[:, q * 128:(q + 1) * 128], kdd[:, h0 + q, :], vn[:, h0 + q, :], True, True) for q in range(4)],
                 r=[B["kdd"], B["vn"]], w=[PSB[bdS]])
        S.add("pool", lambda h: h.tensor_tensor(out=Sd[:, HS, :], in0=Sd[:, HS, :], in1=bc_last(R(4)[:, HS], 128), op=ALU.mult),
              r=[B_Sdh, B_sca], w=[B_Sdh])
        S.add("dve", lambda h: h.tensor_tensor(out=fl(Sd), in0=psf(bdS), in1=fl(Sd), op=ALU.add),
              r=[PSB[bdS], B_Sdh], w=[B_Sdh])
        S.add("act", lambda h: h.activation(out=Sdb[:, HS, :], in_=Sd[:, HS, :], func=AF.Copy), r=[B_Sdh], w=[B_Sdbh])
        out_path(usb[:, FS], [B["usb"]], 4, 128, zs[:, i, FS], B_zs, small[:, l, 28 + h0:32 + h0], 1, i, half=hf)

    def dn_loop(l, ntok, nt, first):
        dk = 128
        if first:
            S.add("dve", lambda h: h.memset(Sd[:], 0.0), w=B_Sd2)
        else:
            S.dma("sp", lambda h: h.dma_start(out=Sd[:].rearrange("p a b -> p (a b)"), in_=sst_d[l, 1]), B_Sd2[0],
                  r=[B_Sdd[l]], w=B_Sd2)
        S.add("act", lambda h: h.activation(out=Sdb[:], in_=Sd[:], func=AF.Copy), r=B_Sd2, w=B_Sdb2)
        RA = lambda k: sca4[:, k, 0:nt, :]
        S.add("act", lambda h: h.activation(out=RA(0), in_=ba[:, 0:nt, 0:8], func=AF.Exp, scale=-1.0),
              r=[B_ba], w=[B_sca])
        S.add("dve", lambda h: h.tensor_scalar(out=RA(0), in0=RA(0), scalar1=1.0, scalar2=None, op0=ALU.add),
              r=[B_sca], w=[B_sca])
        S.add("dve", lambda h: h.reciprocal(out=RA(0), in_=RA(0)), r=[B_sca], w=[B_sca])
        S.add("dve", lambda h: h.tensor_tensor(out=RA(1), in0=ba[:, 0:nt, 8:16], in1=bc_mid(small[:, l, 140:148], nt), op=ALU.add),
              r=[B_ba, B_small], w=[B_sca])
        S.add("act", lambda h: h.activation(out=RA(1), in_=RA(1), func=AF.Exp), r=[B_sca], w=[B_sca])
        S.add("act", lambda h: h.activation(out=RA(1), in_=RA(1), func=AF.Ln, bias=onec[:, 0:1], scale=1.0),
              r=[B_sca, B_c1], w=[B_sca])
        S.add("dve", lambda h: h.tensor_tensor(out=lab4[:, 0:nt, :], in0=RA(1), in1=bc_mid(negA[:, l, :], nt), op=ALU.mult),
              r=[B_sca, B_neg], w=[B_lab])
        bg = PSA.get1()
        items = []
        for i in range(nt):
            items.append((psf(bg)[:, 16 * i:16 * i + 8], U_b, lab4[:, i, :], True, True))
            items.append((psf(bg)[:, 16 * i + 8:16 * i + 16], ones_b, lab4[:, i, :], True, True))
        mm_group(items, r=[B_cmb, B_lab], w=[PSB[bg]])
        pgv = psf(bg)[:, 0:16 * nt].rearrange("p (i a b) -> p a i b", i=nt, a=2)
        S.add("act", lambda h: h.activation(out=sca4[:, 6:8, 0:nt, :], in_=pgv, func=AF.Copy), r=[PSB[bg]], w=[B_sca])
        S.add("act", lambda h: h.activation(out=RA(2), in_=RA(6), func=AF.Exp), r=[B_sca], w=[B_sca])
        S.add("act", lambda h: h.activation(out=RA(4), in_=RA(7), func=AF.Exp), r=[B_sca], w=[B_sca])
        S.add("dve", lambda h: h.tensor_tensor(out=RA(3), in0=RA(7), in1=RA(6), op=ALU.subtract), r=[B_sca], w=[B_sca])
        S.add("act", lambda h: h.activation(out=RA(3), in_=RA(3), func=AF.Exp), r=[B_sca], w=[B_sca])
        S.add("dve", lambda h: h.tensor_tensor(out=RA(5), in0=RA(0), in1=RA(2), op=ALU.mult), r=[B_sca], w=[B_sca])
        S.add("dve", lambda h: h.tensor_scalar(out=RA(8), in0=RA(2), scalar1=float(dk) ** -0.5, scalar2=None,
                                               op0=ALU.mult), r=[B_sca], w=[B_sca])
        for i in range(nt):
            tsl = slice(128 * i, 128 * i + 128)
            R = (lambda i: (lambda k: sca4[:, k, i, :]))(i)
            S.add("pool", lambda h, i=i: h.tensor_tensor(out=RU[:], in0=bc_mid(cmb[:, 1, :], 8), in1=bc_last(lab4[:, i, :], 128),
                                                         op=ALU.mult), r=[B_cmb, B_lab], w=[B_RU])
            S.add("pool", lambda h, i=i: h.tensor_tensor(out=RSL[:], in0=bc_mid(cmb[:, 2, :], 8), in1=bc_last(lab4[:, i, :], 128),
                                                         op=ALU.mult), r=[B_cmb, B_lab], w=[B_RSL])
            halves = []
            for hf in range(2):
                S.begin_defer()
                PSA.mode("h%d" % hf)
                dn_half(l, i, hf, tsl, R)
                halves.append(S.end_defer())
            PSA.mode("all")
            (S.replay_interleaved(halves[0], halves[1]) if KV == 0 else [S._real(it) for hh in halves for it in hh])
        S.dma("sp", lambda h: h.dma_start(out=sst_d[l, 1], in_=Sd[:].rearrange("p a b -> p (a b)")), B_Sd2[0],
              r=B_Sd2, w=[B_Sdd[l]])

    def merge_gla(l, ntok):
        for g in range(2):
            wb, wt = wload(l, f"gg{g}")
            for c in range(4):
                bank = PSA.get1()
                proj_F(wb, wt, c * 128, 128, uT, B_u, KC, ntok, bank)
                S.add("act", lambda h, g=g, c=c, bank=bank: h.activation(
                    out=mrg[:, g * 4 + c, 0:ntok], in_=psf(bank)[:, 0:ntok], func=AF.Sigmoid),
                    r=[PSB[bank]], w=[B_mrg])
        for g in range(2):
            wb, wt = wload(l, f"bg{g}")
            for c in range(4):
                j = g * 4 + c
                bank = PSA.get1()
                proj_F(wb, wt, c * 128, 128, oT[0], B_oT[0], KC, ntok, bank)
                S.add("dve", lambda h, j=j, bank=bank: h.tensor_tensor(
                    out=mrg[:, j, 0:ntok], in0=psf(bank)[:, 0:ntok], in1=mrg[:, j, 0:ntok], op=ALU.mult),
                    r=[PSB[bank], B_mrg], w=[B_mrg])

    def merge_dn(l, ntok):
        for g in range(2):
            wb, wt = wload(l, f"gd{g}")
            for c in range(4):
                bank = PSA.get1()
                proj_F(wb, wt, c * 128, 128, uT, B_u, KC, ntok, bank)
                S.add("act", lambda h, g=g, c=c, bank=bank: h.activation(
                    out=sg2[:, g * 4 + c, 0:ntok], in_=psf(bank)[:, 0:ntok], func=AF.Sigmoid),
                    r=[PSB[bank]], w=[B_sg2])
        for g in range(2):
            wb, wt = wload(l, f"bd{g}")
            for c in range(4):
                j = g * 4 + c
                bank = PSA.get1()
                proj_F(wb, wt, c * 128, 128, oT[1], B_oT[1], KC, ntok, bank)
                S.add("dve", lambda h, j=j, bank=bank: h.tensor_tensor(
                    out=sg2[:, j, 0:ntok], in0=psf(bank)[:, 0:ntok], in1=sg2[:, j, 0:ntok], op=ALU.mult),
                    r=[PSB[bank], B_sg2], w=[B_sg2])

    def merge_phase(l, ntok):
        S.add("pool", lambda h: h.tensor_tensor(out=mrg[:, :, 0:ntok], in0=mrg[:, :, 0:ntok], in1=sg2[:, :, 0:ntok],
                                                op=ALU.add), r=[B_mrg, B_sg2], w=[B_mrg])
        for g in range(2):
            wb, wt = wload(l, f"wo{g}")
            for c in range(4):
                j = g * 4 + c
                bank = PSA.get1()
                proj_F(wb, wt, c * 128, 128, mrg, B_mrg, KC, ntok, bank)
                S.add("dve", lambda h, j=j, bank=bank: h.tensor_tensor(
                    out=hT[:, j, 0:ntok], in0=psf(bank)[:, 0:ntok], in1=hT[:, j, 0:ntok], op=ALU.add),
                    r=[PSB[bank], B_h], w=[B_h])

    def mlp_phase(l, ntok):
        norm_F(sm(l, 8, 16), ntok)
        for hh in range(2):
            for g in range(4):
                wb, wt = wload(l, f"up{hh * 4 + g}")
                for c in range(4):
                    j = g * 4 + c
                    bank = PSA.get1()
                    proj_F(wb, wt, c * 128, 128, uT, B_u, KC, ntok, bank)
                    S.add("act", lambda h, j=j, bank=bank: h.activation(out=h2T[:, j, 0:ntok], in_=psf(bank)[:, 0:ntok],
                                                                        func=AF.Relu), r=[PSB[bank]], w=[B_big])
                    S.add("pool", lambda h, j=j: h.tensor_tensor(out=h2T[:, j, 0:ntok], in0=h2T[:, j, 0:ntok],
                                                                 in1=h2T[:, j, 0:ntok], op=ALU.mult), r=[B_big], w=[B_big])
            for g in range(4):
                wb, wt = wload(l, f"dn{hh}_{g}")
                for c in range(2):
                    j = g * 2 + c
                    bank = PSA.get1()
                    proj_F(wb, wt, c * 128, 128, h2T, B_big, 16, ntok, bank)
                    S.add("dve", lambda h, j=j, bank=bank: h.tensor_tensor(
                        out=hT[:, j, 0:ntok], in0=psf(bank)[:, 0:ntok], in1=hT[:, j, 0:ntok], op=ALU.add),
                        r=[PSB[bank], B_h], w=[B_h])

    xcnt = [0]
    for blk in range(nblocks):
        CUR["blk"] = blk
        nt = 1 if blk == 0 else 4
        ntok = 128 * nt
        for i in range(nt):
            xs = xcnt[0] % 2
            xcnt[0] += 1
            if blk == 0:
                S.add("dve", lambda h, xs=xs: h.memset(xin[xs][:], 0.0), w=B_xin_l[xs])
                S.dma("sp", lambda h, xs=xs: h.dma_start(out=xin[xs][128 - NMETA:128, :], in_=meta_d), B_xin[xs],
                      w=B_xin_l[xs])
            else:
                r0 = (blk - 1) * TB + 128 * i
                S.dma("sp", lambda h, xs=xs, r0=r0: h.dma_start(out=xin[xs][:], in_=x_d[r0:r0 + 128, :]), B_xin[xs],
                      w=B_xin_l[xs])
            for half in range(2):
                bank = PSA.get1()
                tr_group([(psf(bank)[:, c * 128:(c + 1) * 128], xin[xs][:, (half * 4 + c) * 128:(half * 4 + c + 1) * 128],
                           ident_f) for c in range(4)], r=B_xin_l[xs] + [B_cm], w=[PSB[bank]])
                S.add("act", lambda h, half=half, i=i, bank=bank: h.activation(
                    out=hT[:, half * 4:half * 4 + 4, 128 * i:128 * i + 128],
                    in_=psf(bank).rearrange("p (c t) -> p c t", c=4), func=AF.Copy), r=[PSB[bank]], w=[B_h])
        STOP = KSTOP
        for l in range(depth):
            if STOP >= 1:
                norm_F(sm(l, 0, 8), ntok)
            gla_front(l, ntok, nt)
            S.begin_defer()
            PSA.mode("dn")
            gla_loop(l, ntok, nt, blk == 0)
            A_ops = S.end_defer()
            S.begin_defer()
            PSA.mode("gla")
            dn_conv(l, ntok, nt)
            B_ops = S.end_defer()
            PSA.mode("all")
            S.replay_interleaved(A_ops, B_ops)
            merge_gla(l, ntok)
            dn_front(l, ntok, nt)
            dn_loop(l, ntok, nt, blk == 0)
            merge_dn(l, ntok)
            if STOP >= 4:
                merge_phase(l, ntok)
            if STOP >= 5:
                mlp_phase(l, ntok)
        tap("hT", hT[:], [B_h], blk == nblocks - 1)
        if blk > 0:
            for i in range(nt):
                ys = xcnt[0] % 2
                xcnt[0] += 1
                bp = PSA.get2()
                tr_group([(ps_t[:, bp * 512 + c * 128: bp * 512 + (c + 1) * 128], hT[:, c, 128 * i:128 * i + 128], ident_f)
                          for c in range(8)], r=[B_h, B_cm], w=[PSB[bp], PSB[bp + 1]])
                S.add("act", lambda h, bp=bp: h.activation(out=onf[:], in_=ps_t[:, bp * 512:bp * 512 + 1024],
                                                           func=AF.Square, accum_out=oss[:, 0:1]),
                      r=[PSB[bp], PSB[bp + 1]], w=B_onf_h + B_oss_h, attach=False)
                S.add("act", lambda h: h.activation(out=orr[:, 0:1], in_=oss[:, 0:1], func=AF.Ln, bias=epsc[:, 0:1],
                                                    scale=1.0 / D), r=B_oss_h + [B_c1], w=B_orr_h)
                S.add("act", lambda h: h.activation(out=orr[:, 0:1], in_=orr[:, 0:1], func=AF.Exp, scale=-0.5),
                      r=B_orr_h, w=B_orr_h)
                S.add("dve", lambda h, bp=bp, ys=ys: h.scalar_tensor_tensor(
                    out=yout[ys][:], in0=ps_t[:, bp * 512:bp * 512 + 1024], scalar=orr[:, 0:1], in1=wfin[:],
                    op0=ALU.mult, op1=ALU.mult), r=[PSB[bp], PSB[bp + 1], B_wfin] + B_orr_h, w=B_xin_l[ys])
                r0 = (blk - 1) * TB + 128 * i
                S.dma("sp", lambda h, ys=ys, r0=r0: h.dma_start(out=out_d[r0:r0 + 128, :], in_=yout[ys][:]), B_yout[ys],
                      r=B_xin_l[ys])
    S.final_wait("sp", B_xin_l[0] + B_xin_l[1] + [B_Sg] + B_Sd2 + B_wr + [b for bl in B_scr for b in bl] + list(taps.values()))

    keys = S.semkeys()
    sems = {}
    for i, k in enumerate(keys):
        sems[k] = es.enter_context(nc.semaphore(f"s{i}"))
    with nc.Block() as block:
        @block.tensor
        def _(h):
            S.emit("pe", h, sems)

        @block.scalar
        def _(h):
            S.emit("act", h, sems)

        @block.vector
        def _(h):
            S.emit("dve", h, sems)

        @block.gpsimd
        def _(h):
            S.emit("pool", h, sems)

        @block.sync
        def _(h):
            S.emit("sp", h, sems)
    es.close()
    return nc, S


def make_consts():
    cm = np.zeros((128, 20, 128), np.float32)
    r = np.arange(128)[:, None]
    c = np.arange(128)[None, :]
    cm[:, 0] = (r == c)
    cm[:, 1] = (r <= c)
    cm[:, 2] = (r > c)
    cm[:, 3] = (r <= c) * np.float32(128.0 ** -0.5)
    cm[:, 4] = 1.0
    for j in range(7):
        b = 1 << j
        cm[:, 6 + j] = ((r // (2 * b)) == (c // (2 * b))) & ((r % (2 * b)) >= b) & ((c % (2 * b)) < b)
    cm[:, 5] = cm[:, 6].T
    for j in range(7):
        cm[:, 13 + j] = cm[:, 6 + j].T
    return cm


def pack_small(inp, depth):
    sm = np.zeros((depth, 128, NS), np.float32)
    for l in range(depth):
        sm[l, :, 0:8] = inp["mixer_norm_w"][l].reshape(8, 128).T
        sm[l, :, 8:16] = inp["mlp_norm_w"][l].reshape(8, 128).T
        sm[l, :, 16:20] = inp["gla_b_gate"][l].reshape(4, 128).T
        gn = inp["gla_norm_w"][l].reshape(2, 128).T
        sm[l, :, 20:28] = np.tile(gn, (1, 4))
        sm[l, :, 28:36] = np.tile(inp["dn_norm_w"][l].reshape(128, 1), (1, 8))
        cw = inp["dn_conv_w"][l]
        sm[l, :, 36:132] = cw.reshape(4, 24, 128).transpose(2, 1, 0).reshape(128, 96)
        sm[l, :, 132:140] = np.broadcast_to(inp["dn_a_log"][l][None, :], (128, 8))
        sm[l, :, 140:148] = np.broadcast_to(inp["dn_dt_bias"][l][None, :], (128, 8))
    return sm


_CACHE = {}


def run(inputs, depth=DEPTH, nblocks=9, ncores=NCORES, dbg=False, trace=False):
    key = (depth, nblocks, dbg)
    if key not in _CACHE:
        _CACHE[key] = build_program(depth, nblocks, dbg)
    nc, S = _CACHE[key]
    f = lambda a: np.ascontiguousarray(np.asarray(a, dtype=np.float32))
    inp = {k: np.asarray(v) for k, v in inputs.items()}
    nreal = (nblocks - 1) * TB
    shared = {
        "meta": f(inp["meta_tokens"]),
        "w_in": f(inp["w_in"][:depth]),
        "w_branch_gla": f(inp["w_branch_gla"][:depth]),
        "w_branch_dn": f(inp["w_branch_dn"][:depth]),
        "w_out": f(inp["w_out"][:depth]),
        "w_mlp_up": f(inp["w_mlp_up"][:depth]),
        "w_mlp_down": f(inp["w_mlp_down"][:depth]),
        "small": pack_small(inp, depth),
        "gup": f(inp["gla_w_gate_up"][:depth]),
        "cmask": make_consts(),
        "wfin": f(np.broadcast_to(inp["final_norm_w"][None, :], (128, D))),
    }
    in_maps = []
    for c in range(ncores):
        m = dict(shared)
        m["x"] = f(inp["x"][c, :max(nreal, 128)])
        in_maps.append(m)
    res = run_bass_kernel_spmd(nc, in_maps, core_ids=list(range(ncores)), trace=trace)
    return res


def kernel(**inputs):
    res = run(inputs)
    out = np.stack([np.asarray(r["out"], dtype=np.float32) for r in res.results], axis=0)
    return out
```
